# Optimizing a Trainium2 kernel written in Bass

```python
import math
import jax
import jax.numpy as jnp
from jax import lax
import numpy as np

D_MODEL = 4096
BATCH = 2
SEQ = 8192
DEPTH = 2

GRID_W = 64
CTX_LEN = 256
N_BRANCH = 4
BRANCH_W = D_MODEL // 4
SHORT_K = 3
EPS = 1e-6
ALPHA = (2 * DEPTH) ** 0.25
BETA = (8 * DEPTH) ** -0.25

HY_W = BRANCH_W
HY_ORDER = 2
HY_BANDS = 16
HY_EMB = 2 * HY_BANDS + 1
HY_HID = 64
HY_TARGET = 1e-2
HY_FAST = 0.3
HY_SLOW = 1.5

SSD_W = BRANCH_W
SSD_HEADDIM = 64
SSD_HEADS = SSD_W // SSD_HEADDIM
SSD_GROUPS = 4
SSD_RPG = SSD_HEADS // SSD_GROUPS
SSD_STATE = 128
SSD_CHUNK = 128
SSD_XBC = SSD_W + 2 * SSD_GROUPS * SSD_STATE

GLA_HEADS = 4
GLA_DV = BRANCH_W
GLA_DK = BRANCH_W // 2
GLA_HDK = GLA_DK // GLA_HEADS
GLA_HDV = GLA_DV // GLA_HEADS
GLA_RANK = 16
GLA_NORMALIZER = 16.0
GLA_CHUNK = 64

RET_HEADS = 4
RET_DK = BRANCH_W
RET_DV = BRANCH_W
RET_HDK = RET_DK // RET_HEADS
RET_HDV = RET_DV // RET_HEADS
RET_CHUNK = 128
ROPE_BASE = 10000.0

IN_SIZES = (3 * HY_W, HY_W,
            SSD_XBC, 2 * SSD_HEADS, SSD_W,
            GLA_DK, GLA_DK, GLA_DV, 2 * GLA_RANK, GLA_DV,
            RET_DK, RET_DK, RET_DV, RET_DV)
N_IN = sum(IN_SIZES)

kernel_name = 'hybrid_bidir_hyena_ssd_gla_retnet'


def layer_norm(x):
    x32 = x.astype(jnp.float32)
    xc = x32 - jnp.mean(x32, axis=-1, keepdims=True)
    return xc * lax.rsqrt(jnp.mean(xc * xc, axis=-1, keepdims=True) + EPS)


def rms_norm(x):
    x32 = x.astype(jnp.float32)
    return x32 * lax.rsqrt(jnp.mean(x32 * x32, axis=-1, keepdims=True) + EPS)


def split_cols(p, sizes):
    offsets = np.cumsum(np.array(sizes))[:-1]
    return jnp.split(p, [int(o) for o in offsets], axis=-1)


def short_conv(u, w, b):
    y = lax.conv_general_dilated(u, w[:, None, :].astype(u.dtype), window_strides=(1,),
                                 padding=[(SHORT_K // 2, SHORT_K // 2)],
                                 dimension_numbers=('NWC', 'WIO', 'NWC'),
                                 feature_group_count=u.shape[-1])
    return y + b.astype(u.dtype)


def chunk_scan(local, decay, s0):
    if s0 is None:
        s0 = jnp.zeros_like(local[:, 0])

    def step(s, inp):
        l, d = inp
        return d * s + l, s

    final, starts = lax.scan(step, s0, (jnp.moveaxis(local, 1, 0), jnp.moveaxis(decay, 1, 0)))
    return jnp.moveaxis(starts, 0, 1), final


def bidirectional(run, ctx_ins, lat_ins, params):
    y_ctx, y_lat = [], []
    for d in range(2):
        f = (lambda a: jnp.flip(a, axis=1)) if d == 1 else (lambda a: a)
        yc, s_ctx = run(*[f(a) for a in ctx_ins[d]], *params[d], None)
        yl, _ = run(*[f(a) for a in lat_ins[d]], *params[d], s_ctx)
        y_ctx.append(f(yc))
        y_lat.append(f(yl))
    return y_ctx[0] + y_ctx[1], y_lat[0] + y_lat[1]


def hyena_filter_spectrum(L, w1, b1, w2, b2, w3, b3, w4, freq):
    f32 = jnp.float32
    w1, b1, w2, b2, w3, b3, w4, freq = [a.astype(f32) for a in (w1, b1, w2, b2, w3, b3, w4, freq)]
    t = jnp.arange(L, dtype=f32)[:, None] / L
    bands = jnp.arange(1, HY_BANDS + 1, dtype=f32)[None, :]
    z = jnp.concatenate([t, jnp.cos(2.0 * math.pi * bands * t), jnp.sin(2.0 * math.pi * bands * t)], axis=-1)
    hdn = jnp.sin(freq[0] * (z @ w1 + b1))
    hdn = jnp.sin(freq[1] * (hdn @ w2 + b2))
    hdn = jnp.sin(freq[2] * (hdn @ w3 + b3))
    filt = (hdn @ w4).reshape(L, HY_ORDER, 2, HY_W)
    deltas = jnp.linspace(math.log(HY_TARGET) / HY_SLOW, math.log(HY_TARGET) / HY_FAST, HY_W)
    window = jnp.exp(-t * jnp.abs(deltas)[None, :])
    filt = filt * window[:, None, None, :]
    fwd = filt[:, :, 0]
    bwd = filt[1:, :, 1]
    circ = jnp.concatenate([fwd, jnp.zeros((1, HY_ORDER, HY_W), f32), jnp.flip(bwd, axis=0)], axis=0)
    return jnp.fft.rfft(circ, n=2 * L, axis=0)


def fft_long_conv(u, spec):
    L = u.shape[1]
    uf = jnp.fft.rfft(u, n=2 * L, axis=1)
    return jnp.fft.irfft(uf * spec[None], n=2 * L, axis=1)[:, :L]


def hyena_seq(u_in, gate, conv_w, conv_b, filt_params, skip):
    L = u_in.shape[1]
    u = short_conv(u_in, conv_w, conv_b).astype(jnp.float32)
    parts = jnp.split(u, HY_ORDER + 1, axis=-1)
    spec = hyena_filter_spectrum(L, *filt_params)
    skip = skip.astype(jnp.float32)
    z = parts[0]
    for o in range(HY_ORDER):
        z = parts[o + 1] * (fft_long_conv(z, spec[:, o]) + z * skip[o])
    return z * jax.nn.silu(gate.astype(jnp.float32))


def ssd_scan(x, bm, cm, dt, a_log, s0):
    b, L = x.shape[:2]
    q = SSD_CHUNK
    nc = L // q
    x = x.reshape(b, nc, q, SSD_GROUPS, SSD_RPG, SSD_HEADDIM)
    bm = bm.reshape(b, nc, q, SSD_GROUPS, SSD_STATE)
    cm = cm.reshape(b, nc, q, SSD_GROUPS, SSD_STATE)
    dt = dt.reshape(b, nc, q, SSD_GROUPS, SSD_RPG)
    a_cum = jnp.cumsum(dt * (-jnp.exp(a_log)), axis=2)
    xdt = x * dt[..., None]
    tri = jnp.tril(jnp.ones((q, q), dtype=bool))[None, None, :, :, None, None]
    seg = a_cum[:, :, :, None] - a_cum[:, :, None, :]
    decay_ij = jnp.exp(jnp.where(tri, seg, -jnp.inf))
    scores = jnp.einsum('bcign,bcjgn->bcijg', cm, bm)
    y = jnp.einsum('bcijgr,bcjgrp->bcigrp', scores[..., None] * decay_ij, xdt)
    a_last = a_cum[:, :, -1:]
    local = jnp.einsum('bcjgn,bcjgrp->bcgrpn', bm, xdt * jnp.exp(a_last - a_cum)[..., None])
    starts, final = chunk_scan(local, jnp.exp(a_last[:, :, 0])[..., None, None], s0)
    y = y + jnp.einsum('bcign,bcgrpn->bcigrp', cm, starts) * jnp.exp(a_cum)[..., None]
    return y.reshape(b, L, SSD_GROUPS, SSD_RPG, SSD_HEADDIM), final


def ssd_prep(xbc, dt_raw, conv_w, conv_b, dt_bias):
    b, L = xbc.shape[:2]
    xbc = jax.nn.silu(short_conv(xbc, conv_w, conv_b).astype(jnp.float32))
    xs, bm, cm = split_cols(xbc, (SSD_W, SSD_GROUPS * SSD_STATE, SSD_GROUPS * SSD_STATE))
    xs = xs.reshape(b, L, SSD_GROUPS, SSD_RPG, SSD_HEADDIM)
    bm = bm.reshape(b, L, SSD_GROUPS, SSD_STATE)
    cm = cm.reshape(b, L, SSD_GROUPS, SSD_STATE)
    dt = jax.nn.softplus(dt_raw.astype(jnp.float32).reshape(b, L, 2, SSD_HEADS) + dt_bias)
    return xs, bm, cm, dt.reshape(b, L, 2, SSD_GROUPS, SSD_RPG)


def ssd_mixer(parts_c, parts_l, conv_w, conv_b, a_log, dt_bias, d_skip, norm_w):
    dt_bias = dt_bias.astype(jnp.float32)
    pc = ssd_prep(parts_c[0], parts_c[1], conv_w, conv_b, dt_bias)
    pl = ssd_prep(parts_l[0], parts_l[1], conv_w, conv_b, dt_bias)
    a_log = a_log.astype(jnp.float32).reshape(2, SSD_GROUPS, SSD_RPG)

    def dir_in(p, d):
        return (p[0], p[1], p[2], p[3][:, :, d])

    yc, yl = bidirectional(ssd_scan, (dir_in(pc, 0), dir_in(pc, 1)), (dir_in(pl, 0), dir_in(pl, 1)),
                           ((a_log[0],), (a_log[1],)))
    d = d_skip.astype(jnp.float32).reshape(SSD_GROUPS, SSD_RPG, 1)

    def finish(y, xs, z):
        b, L = y.shape[:2]
        y = (y + d * xs).reshape(b, L, SSD_W) * jax.nn.silu(z.astype(jnp.float32))
        y = rms_norm(y.reshape(b, L, SSD_GROUPS, SSD_W // SSD_GROUPS)).reshape(b, L, SSD_W)
        return y * norm_w.astype(jnp.float32)

    return finish(yc, pc[0], parts_c[2]), finish(yl, pl[0], parts_l[2])


def gla_scan(q, k, v, g, s0):
    b, L, h, dk = q.shape
    dv = v.shape[-1]
    cs = GLA_CHUNK
    nc = L // cs
    q = q.reshape(b, nc, cs, h, dk)
    k = k.reshape(b, nc, cs, h, dk)
    v = v.reshape(b, nc, cs, h, dv)
    gc = jnp.cumsum(g.reshape(b, nc, cs, h, dk), axis=2)
    g_mid = gc[:, :, cs // 2:cs // 2 + 1]
    g_last = gc[:, :, -1:]
    att = jnp.einsum('bcihk,bcjhk->bchij', q * jnp.exp(gc - g_mid), k * jnp.exp(g_mid - gc))
    att = jnp.where(jnp.tril(jnp.ones((cs, cs), dtype=bool)), att, 0.0)
    y = jnp.einsum('bchij,bcjhv->bcihv', att, v)
    local = jnp.einsum('bcjhk,bcjhv->bchkv', k * jnp.exp(g_last - gc), v)
    starts, final = chunk_scan(local, jnp.exp(g_last[:, :, 0])[..., None], s0)
    y = y + jnp.einsum('bcihk,bchkv->bcihv', q * jnp.exp(gc), starts)
    return y.reshape(b, L, h, dv), final


def gla_mixer(parts_c, parts_l, w2, b2, norm_w):
    f32 = jnp.float32

    def prep(p):
        q, k, v, lr, _ = p
        b, L = q.shape[:2]
        q = q.astype(f32).reshape(b, L, GLA_HEADS, GLA_HDK) * GLA_HDK ** -0.5
        k = k.astype(f32).reshape(b, L, GLA_HEADS, GLA_HDK)
        v = v.astype(f32).reshape(b, L, GLA_HEADS, GLA_HDV)
        lr = lr.astype(f32)
        gs = []
        for d in range(2):
            logit = lr[..., d * GLA_RANK:(d + 1) * GLA_RANK] @ w2[d].astype(f32) + b2[d].astype(f32)
            gs.append((jax.nn.log_sigmoid(logit) / GLA_NORMALIZER).reshape(b, L, GLA_HEADS, GLA_HDK))
        return (q, k, v, gs[0]), (q, k, v, gs[1])

    yc, yl = bidirectional(gla_scan, prep(parts_c), prep(parts_l), ((), ()))

    def finish(y, g):
        b, L = y.shape[:2]
        y = (rms_norm(y) * norm_w.astype(f32)).reshape(b, L, GLA_DV)
        return y * jax.nn.silu(g.astype(f32))

    return finish(yc, parts_c[4]), finish(yl, parts_l[4])


def rope_1d(x, pos):
    d = x.shape[-1]
    inv = ROPE_BASE ** (-jnp.arange(0, d, 2, dtype=jnp.float32) / d)
    ang = pos.astype(jnp.float32)[:, None] * inv[None, :]
    cos, sin = jnp.cos(ang)[:, None, :], jnp.sin(ang)[:, None, :]
    x1, x2 = x[..., :d // 2], x[..., d // 2:]
    return jnp.concatenate([x1 * cos - x2 * sin, x2 * cos + x1 * sin], axis=-1)


def rope_2d(x):
    L = x.shape[1]
    rows = L // GRID_W
    row = jnp.broadcast_to(jnp.arange(rows)[:, None], (rows, GRID_W)).reshape(L)
    col = jnp.broadcast_to(jnp.arange(GRID_W)[None, :], (rows, GRID_W)).reshape(L)
    half = x.shape[-1] // 2
    return jnp.concatenate([rope_1d(x[..., :half], row), rope_1d(x[..., half:], col)], axis=-1)


def retention_scan(q, k, v, lam, s0):
    b, L, h, dk = q.shape
    dv = v.shape[-1]
    cs = RET_CHUNK
    nc = L // cs
    q = q.reshape(b, nc, cs, h, dk)
    k = k.reshape(b, nc, cs, h, dk)
    v = v.reshape(b, nc, cs, h, dv)
    pos = jnp.arange(cs, dtype=jnp.float32)
    lag = pos[:, None] - pos[None, :]
    dmat = jnp.where(lag >= 0, jnp.exp(lam[:, None, None] * jnp.maximum(lag, 0.0)), 0.0)
    att = jnp.einsum('bcihk,bcjhk->bchij', q, k) * dmat
    y = jnp.einsum('bchij,bcjhv->bcihv', att, v)
    to_end = jnp.exp(lam[None, :] * (cs - 1.0 - pos)[:, None])
    local = jnp.einsum('bcjhk,bcjhv->bchkv', k * to_end[:, :, None], v)
    decay = jnp.broadcast_to(jnp.exp(lam * cs)[None, None, :, None, None], (1, nc, h, 1, 1))
    starts, final = chunk_scan(local, decay, s0)
    from_start = jnp.exp(lam[None, :] * (pos + 1.0)[:, None])
    y = y + jnp.einsum('bcihk,bchkv->bcihv', q, starts) * from_start[:, :, None]
    return y.reshape(b, L, h, dv), final


def ret_mixer(parts_c, parts_l, decay_raw):
    f32 = jnp.float32
    lam = -jnp.exp(decay_raw.astype(f32))

    def prep(p, use_rope):
        q, k, v, _ = p
        b, L = q.shape[:2]
        q = q.astype(f32).reshape(b, L, RET_HEADS, RET_HDK)
        k = k.astype(f32).reshape(b, L, RET_HEADS, RET_HDK) * RET_HDK ** -0.5
        v = v.astype(f32).reshape(b, L, RET_HEADS, RET_HDV)
        if use_rope:
            q, k = rope_2d(q), rope_2d(k)
        return (q, k, v)

    ic = prep(parts_c, False)
    il = prep(parts_l, True)
    yc, yl = bidirectional(retention_scan, (ic, ic), (il, il), ((lam[0],), (lam[1],)))

    def finish(y, g):
        b, L = y.shape[:2]
        return layer_norm(y).reshape(b, L, RET_DV) * jax.nn.silu(g.astype(f32))

    return finish(yc, parts_c[3]), finish(yl, parts_l[3])


def modulate_project(s, mod, w_in):
    shift, scale, _ = mod
    h = (layer_norm(s) * (1.0 + scale) + shift).astype(s.dtype)
    return h, split_cols(h @ w_in, IN_SIZES)


def post_residual(s, h, ys, gate, lp):
    m = None
    for i, y in enumerate(ys):
        term = jax.nn.sigmoid(h @ lp['w_gate'][i]) * (y.astype(h.dtype) @ lp['w_br'][i])
        m = term if m is None else m + term
    out = m @ lp['w_out']
    y = layer_norm(ALPHA * s + gate * out) * lp['ln_g'] + lp['ln_b']
    return y.astype(s.dtype)


def trunk_layer(x, ctx, mod_x, mod_c, lp):
    hx, px = modulate_project(x, mod_x, lp['w_in'])
    hc, pc = modulate_project(ctx, mod_c, lp['w_in'])
    filt = (lp['hy_w1'], lp['hy_b1'], lp['hy_w2'], lp['hy_b2'], lp['hy_w3'], lp['hy_b3'], lp['hy_w4'], lp['hy_freq'])
    ys_c = [hyena_seq(pc[0], pc[1], lp['hy_conv_w'], lp['hy_conv_b'], filt, lp['hy_skip'])]
    ys_x = [hyena_seq(px[0], px[1], lp['hy_conv_w'], lp['hy_conv_b'], filt, lp['hy_skip'])]
    yc, yx = ssd_mixer(pc[2:5], px[2:5], lp['ssd_conv_w'], lp['ssd_conv_b'], lp['ssd_a_log'],
                       lp['ssd_dt_bias'], lp['ssd_d'], lp['ssd_norm_w'])
    ys_c.append(yc)
    ys_x.append(yx)
    yc, yx = gla_mixer(pc[5:10], px[5:10], lp['gla_w2'], lp['gla_b2'], lp['gla_norm_w'])
    ys_c.append(yc)
    ys_x.append(yx)
    yc, yx = ret_mixer(pc[10:14], px[10:14], lp['ret_decay'])
    ys_c.append(yc)
    ys_x.append(yx)
    x_new = post_residual(x, hx, ys_x, mod_x[2], lp)
    ctx_new = post_residual(ctx, hc, ys_c, mod_c[2], lp)
    return x_new, ctx_new


def setup_inputs(seed: int = 0) -> dict:
    key = jax.random.key(seed)
    ks = jax.random.split(key, 40)
    f32 = jnp.float32

    def nrm(i, shape, scale):
        return jax.random.normal(ks[i], shape, f32) * scale

    D = D_MODEL
    dt0 = jnp.exp(jax.random.uniform(ks[21], (DEPTH, 2, SSD_HEADS), f32) * (math.log(1e-1) - math.log(1e-3)) + math.log(1e-3))
    ret_base = jnp.log(-jnp.log1p(-(2.0 ** (-5.0 - jnp.arange(RET_HEADS, dtype=f32)))))
    return {
        'x': nrm(0, (BATCH, SEQ, D), 1.0),
        'c': nrm(1, (BATCH, D), 1.0),
        'ctx': nrm(2, (BATCH, CTX_LEN, D), 1.0),
        'c_ctx': nrm(3, (D,), 1.0),
        'w_ada': nrm(4, (DEPTH, D, 3 * D), 0.5 * D ** -0.5),
        'b_ada': nrm(5, (DEPTH, 3 * D), 0.01),
        'w_in': nrm(6, (DEPTH, D, N_IN), D ** -0.5),
        'hy_conv_w': nrm(7, (DEPTH, SHORT_K, 3 * HY_W), SHORT_K ** -0.5),
        'hy_conv_b': nrm(8, (DEPTH, 3 * HY_W), 0.01),
        'hy_w1': nrm(9, (DEPTH, HY_EMB, HY_HID), HY_EMB ** -0.5),
        'hy_b1': nrm(10, (DEPTH, HY_HID), 0.02),
        'hy_w2': nrm(11, (DEPTH, HY_HID, HY_HID), HY_HID ** -0.5),
        'hy_b2': nrm(12, (DEPTH, HY_HID), 0.02),
        'hy_w3': nrm(13, (DEPTH, HY_HID, HY_HID), HY_HID ** -0.5),
        'hy_b3': nrm(14, (DEPTH, HY_HID), 0.02),
        'hy_w4': nrm(15, (DEPTH, HY_HID, HY_ORDER * 2 * HY_W), 0.1 * HY_HID ** -0.5),
        'hy_freq': 1.0 + nrm(16, (DEPTH, 3, HY_HID), 0.01),
        'hy_skip': nrm(17, (DEPTH, HY_ORDER, HY_W), 0.5),
        'ssd_conv_w': nrm(18, (DEPTH, SHORT_K, SSD_XBC), SHORT_K ** -0.5),
        'ssd_conv_b': nrm(19, (DEPTH, SSD_XBC), 0.01),
        'ssd_a_log': jnp.log(jax.random.uniform(ks[20], (DEPTH, 2, SSD_HEADS), f32, 1.0, 16.0)),
        'ssd_dt_bias': dt0 + jnp.log(-jnp.expm1(-dt0)),
        'ssd_d': 1.0 + nrm(22, (DEPTH, SSD_HEADS), 0.01),
        'ssd_norm_w': 1.0 + nrm(23, (DEPTH, SSD_W), 0.01),
        'gla_w2': nrm(24, (DEPTH, 2, GLA_RANK, GLA_DK), GLA_RANK ** -0.5),
        'gla_b2': nrm(25, (DEPTH, 2, GLA_DK), 0.01),
        'gla_norm_w': 1.0 + nrm(26, (DEPTH, GLA_HDV), 0.01),
        'ret_decay': ret_base[None, None, :] + nrm(27, (DEPTH, 2, RET_HEADS), 0.01),
        'w_gate': nrm(28, (DEPTH, N_BRANCH, D, D), D ** -0.5),
        'w_br': nrm(29, (DEPTH, N_BRANCH, BRANCH_W, D), BETA * BRANCH_W ** -0.5),
        'w_out': nrm(30, (DEPTH, D, D), BETA * D ** -0.5),
        'ln_g': 1.0 + nrm(31, (DEPTH, D), 0.01),
        'ln_b': nrm(32, (DEPTH, D), 0.01),
    }


def reference(x, c, ctx, c_ctx, w_ada, b_ada, w_in, hy_conv_w, hy_conv_b, hy_w1, hy_b1, hy_w2, hy_b2,
              hy_w3, hy_b3, hy_w4, hy_freq, hy_skip, ssd_conv_w, ssd_conv_b, ssd_a_log, ssd_dt_bias, ssd_d,
              ssd_norm_w, gla_w2, gla_b2, gla_norm_w, ret_decay, w_gate, w_br, w_out, ln_g, ln_b):
    for l in range(DEPTH):
        mx = jax.nn.silu(c) @ w_ada[l] + b_ada[l]
        mc = jax.nn.silu(c_ctx) @ w_ada[l] + b_ada[l]
        mod_x = [m[:, None, :] for m in jnp.split(mx, 3, axis=-1)]
        mod_c = [m[None, None, :] for m in jnp.split(mc, 3, axis=-1)]
        lp = {
            'w_in': w_in[l], 'hy_conv_w': hy_conv_w[l], 'hy_conv_b': hy_conv_b[l],
            'hy_w1': hy_w1[l], 'hy_b1': hy_b1[l], 'hy_w2': hy_w2[l], 'hy_b2': hy_b2[l],
            'hy_w3': hy_w3[l], 'hy_b3': hy_b3[l], 'hy_w4': hy_w4[l], 'hy_freq': hy_freq[l],
            'hy_skip': hy_skip[l], 'ssd_conv_w': ssd_conv_w[l], 'ssd_conv_b': ssd_conv_b[l],
            'ssd_a_log': ssd_a_log[l], 'ssd_dt_bias': ssd_dt_bias[l], 'ssd_d': ssd_d[l],
            'ssd_norm_w': ssd_norm_w[l], 'gla_w2': gla_w2[l], 'gla_b2': gla_b2[l],
            'gla_norm_w': gla_norm_w[l], 'ret_decay': ret_decay[l], 'w_gate': w_gate[l],
            'w_br': w_br[l], 'w_out': w_out[l], 'ln_g': ln_g[l], 'ln_b': ln_b[l],
        }
        x, ctx = trunk_layer(x, ctx, mod_x, mod_c, lp)
    return x
```

```python
import math
from contextlib import ExitStack
import numpy as np
import ml_dtypes
import concourse.bass as bass
import concourse.mybir as mybir
from concourse.bass_utils import run_bass_kernel_spmd

F32 = mybir.dt.float32
BF16 = mybir.dt.bfloat16
AF = mybir.ActivationFunctionType
ALU = mybir.AluOpType
AX = mybir.AxisListType
NPBF = ml_dtypes.bfloat16

EPS = 1e-6
DEPTH = 2
ALPHA = (2 * DEPTH) ** 0.25
CTX = 256
PI = math.pi


class Buf:
    def __init__(self, t):
        self.t = t
        self.writers = {}
        self.readers = {}

    def __getitem__(self, idx):
        return self.t[idx]


class Ring:
    def __init__(self, bufs):
        self.bufs = bufs
        self.i = 0

    def next(self):
        b = self.bufs[self.i % len(self.bufs)]
        self.i += 1
        return b


class Prog:
    def __init__(self, nc, n_dma_sems=8):
        self.nc = nc
        self.eng = {"pe": nc.tensor, "act": nc.scalar, "dve": nc.vector, "pool": nc.gpsimd, "sp": nc.sync}
        self.sem, self.cnt, self._ctx = {}, {}, []
        for k in ["pe", "act", "dve", "pool"] + ["dma%d" % i for i in range(n_dma_sems)]:
            cm = nc.semaphore("s_" + k)
            self.sem[k] = cm.__enter__()
            self._ctx.append(cm)
            self.cnt[k] = 0
        self.dma_keys = ["dma%d" % i for i in range(n_dma_sems)]
        self.dma_rr = 0
        self.seen = {e: {} for e in self.eng}
        self.n_inst = 0

    def close(self):
        for cm in reversed(self._ctx):
            cm.__exit__(None, None, None)

    def _wait(self, e, deps):
        for k, v in deps.items():
            if v <= 0 or self.seen[e].get(k, 0) >= v:
                continue
            self.eng[e].wait_ge(self.sem[k], v)
            self.seen[e][k] = v

    @staticmethod
    def _deps(reads, writes):
        deps = {}
        for r in reads:
            for k, v in r.writers.items():
                if deps.get(k, 0) < v:
                    deps[k] = v
        for w in writes:
            for d in (w.writers, w.readers):
                for k, v in d.items():
                    if deps.get(k, 0) < v:
                        deps[k] = v
        return deps

    @staticmethod
    def _commit(key, val, reads, writes):
        for r in reads:
            r.readers[key] = val
        for w in writes:
            w.writers[key] = val
            w.readers = {}

    def op(self, e, ins_fn, reads=(), writes=()):
        deps = self._deps(reads, writes)
        if e == "pe":
            deps.pop("pe", None)
        self._wait(e, deps)
        ins = ins_fn()
        self.cnt[e] += 1
        ins.then_inc(self.sem[e], 1)
        self._commit(e, self.cnt[e], reads, writes)
        self.n_inst += 1

    def dma(self, out, in_, reads=(), writes=(), q="sp", **kw):
        k = self.dma_keys[self.dma_rr % len(self.dma_keys)]
        self.dma_rr += 1
        deps = self._deps(reads, writes)
        if self.cnt[k] > 0:
            deps[k] = max(deps.get(k, 0), self.cnt[k])
        self._wait(q, deps)
        ins = self.eng[q].dma_start(out=out, in_=in_, **kw)
        self.cnt[k] += 16
        ins.then_inc(self.sem[k], 16)
        self._commit(k, self.cnt[k], reads, writes)
        self.n_inst += 1

    def finish(self, bufs):
        deps = {}
        for b in bufs:
            for k, v in b.writers.items():
                deps[k] = max(deps.get(k, 0), v)
        self._wait("sp", deps)

    def mm(self, out, lhsT, rhs, start, stop, reads, writes):
        nc = self.nc
        self.op("pe", lambda: nc.tensor.matmul(out, lhsT=lhsT, rhs=rhs, start=start, stop=stop), reads, writes)

    def tr(self, out, in_, ident, reads, writes):
        nc = self.nc
        self.op("pe", lambda: nc.tensor.transpose(out, in_, ident), reads, writes)

    def act(self, out, in_, func, reads, writes, bias=None, scale=None):
        nc = self.nc
        kw = {}
        if bias is not None:
            kw["bias"] = bias
        if scale is not None:
            kw["scale"] = scale
        self.op("act", lambda: nc.scalar.activation(out=out, in_=in_, func=func, **kw), reads, writes)

    def tt(self, e, out, in0, in1, op, reads, writes):
        en = self.eng[e]
        self.op(e, lambda: en.tensor_tensor(out=out, in0=in0, in1=in1, op=op), reads, writes)

    def ts(self, e, out, in0, s1, s2, op0, op1, reads, writes):
        en = self.eng[e]
        if s2 is None:
            self.op(e, lambda: en.tensor_scalar(out=out, in0=in0, scalar1=s1, scalar2=None, op0=op0), reads, writes)
        else:
            self.op(e, lambda: en.tensor_scalar(out=out, in0=in0, scalar1=s1, scalar2=s2, op0=op0, op1=op1),
                    reads, writes)

    def stt(self, e, out, in0, scalar, in1, op0, op1, reads, writes):
        en = self.eng[e]
        self.op(e, lambda: en.scalar_tensor_tensor(out=out, in0=in0, scalar=scalar, in1=in1, op0=op0, op1=op1),
                reads, writes)

    def copy(self, e, out, in_, reads, writes):
        if e == "act":
            self.act(out, in_, AF.Copy, reads, writes)
        else:
            en = self.eng[e]
            self.op(e, lambda: en.tensor_copy(out=out, in_=in_), reads, writes)

    def memset(self, e, buf, ap, val):
        en = self.eng[e]
        self.op(e, lambda: en.memset(ap, val), (), [buf])


class Ctx:
    _N = [0]

    def __init__(self, nc, es):
        self.nc, self.es = nc, es

    def sb(self, shape, dt, name=None):
        Ctx._N[0] += 1
        return Buf(self.es.enter_context(self.nc.sbuf_tensor("%s_%d" % (name or "sb", Ctx._N[0]), list(shape), dt)))

    def ps(self, shape, dt, name=None):
        Ctx._N[0] += 1
        return Buf(self.es.enter_context(self.nc.psum_tensor("%s_%d" % (name or "ps", Ctx._N[0]), list(shape), dt)))

    def din(self, name, shape, dt=F32):
        return self.nc.dram_tensor(name, list(shape), dt, kind="ExternalInput").ap()

    def dout(self, name, shape, dt=F32):
        return self.nc.dram_tensor(name, list(shape), dt, kind="ExternalOutput").ap()

    def dscr(self, name, shape, dt):
        return self.nc.dram_tensor(name, list(shape), dt).ap()


def bc_mid(ap2d, n):
    return ap2d.unsqueeze(1).to_broadcast([ap2d.shape[0], n, ap2d.shape[1]])


def emit_ln_fm(P, C, K, src, T, KC, sc_ap, sh_ap, out_buf, out_ap, sq, ps_ring, st):
    nc = P.nc
    D = 128 * KC
    ones = K["onesf"]
    P.act(sq[:, :, 0:T], src[:, :, 0:T], AF.Square, [src], [sq])
    p1 = ps_ring.next()
    p2 = ps_ring.next()
    for kc in range(KC):
        P.mm(p1[:, 0:T], ones[:, :], src[:, kc, 0:T], kc == 0, kc == KC - 1, [ones, src], [p1])
    for kc in range(KC):
        P.mm(p2[:, 0:T], ones[:, :], sq[:, kc, 0:T], kc == 0, kc == KC - 1, [ones, sq], [p2])
    mean, var, rstd, nmr = st
    P.ts("dve", mean[:, 0:T], p1[:, 0:T], 1.0 / D, None, ALU.mult, None, [p1], [mean])
    P.ts("dve", var[:, 0:T], p2[:, 0:T], 1.0 / D, None, ALU.mult, None, [p2], [var])
    P.tt("dve", rstd[:, 0:T], mean[:, 0:T], mean[:, 0:T], ALU.mult, [mean], [rstd])
    P.tt("dve", var[:, 0:T], var[:, 0:T], rstd[:, 0:T], ALU.subtract, [var, rstd], [var])
    emit_rsqrt(P, rstd, rstd[:, 0:T], var, var[:, 0:T], EPS)
    P.stt("dve", nmr[:, 0:T], mean[:, 0:T], -1.0, rstd[:, 0:T], ALU.mult, ALU.mult, [mean, rstd], [nmr])
    P.tt("dve", src[:, :, 0:T], src[:, :, 0:T], bc_mid(rstd[:, 0:T], KC), ALU.mult, [src, rstd], [src])
    P.tt("dve", src[:, :, 0:T], src[:, :, 0:T], bc_mid(nmr[:, 0:T], KC), ALU.add, [src, nmr], [src])
    for kc in range(KC):
        P.act(out_ap(kc), src[:, kc, 0:T], AF.Identity, [src], [out_buf],
              bias=sh_ap[:, kc:kc + 1], scale=sc_ap[:, kc:kc + 1])


def emit_rsqrt(P, ob, o_ap, ib, i_ap, eps, scale=1.0):
    nc = P.nc
    P.ts("dve", i_ap, i_ap, scale, eps, ALU.mult, ALU.add, [ib], [ib])
    P.act(i_ap, i_ap, AF.Sqrt, [ib], [ib])
    P.op("dve", lambda: nc.vector.reciprocal(out=o_ap, in_=i_ap), [ib], [ob])


def load_consts(P, C, cin, names_f, names_b):
    K = {}
    for nm in names_f:
        ap = cin[nm]
        b = C.sb(ap.shape, F32, nm)
        P.dma(b[:], ap, (), [b])
        K[nm] = b
    for nm in names_b:
        ap = cin[nm]
        b = C.sb(ap.shape, BF16, nm)
        P.dma(b[:], ap, (), [b])
        K[nm] = b
    return K


def build_ada(KC, NCH):
    nc = bass.Bass("TRN2", target_bir_lowering=False)
    with ExitStack() as es:
        C = Ctx(nc, es)
        P = Prog(nc)
        cT = C.din("cT", [128, KC, 4])
        w = C.din("w", [NCH, 128, KC, 128])
        b = C.din("b", [128, NCH])
        o = C.dout("o", [128, NCH, 4])
        ct = C.sb([128, KC, 4], F32)
        sg = C.sb([128, KC, 4], F32)
        st = C.sb([128, KC, 4], F32)
        bt = C.sb([128, NCH], F32)
        ot = C.sb([128, NCH, 4], F32)
        wr = Ring([C.sb([128, KC, 128], F32) for _ in range(3)])
        pr = Ring([C.ps([128, 512], F32) for _ in range(2)])
        odr = Buf(None)
        P.dma(ct[:], cT, (), [ct])
        P.dma(bt[:], b, (), [bt])
        P.act(sg[:], ct[:], AF.Sigmoid, [ct], [sg])
        P.tt("dve", st[:], ct[:], sg[:], ALU.mult, [ct, sg], [st])
        for n in range(NCH):
            wt = wr.next()
            P.dma(wt[:], w[n], (), [wt])
            pp = pr.next()
            for kc in range(KC):
                P.mm(pp[:, 0:4], wt[:, kc, :], st[:, kc, :], kc == 0, kc == KC - 1, [wt, st], [pp])
            P.act(ot[:, n, :], pp[:, 0:4], AF.Identity, [pp, bt], [ot], bias=bt[:, n:n + 1], scale=1.0)
        P.dma(o, ot[:], [ot], [odr], q="pool")
        P.finish([odr])
        P.close()
    return nc


def build_post(KC, NCT, NLT, TP=256):
    D = 128 * KC
    NTK = NCT + NLT
    nc = bass.Bass("TRN2", target_bir_lowering=False)
    with ExitStack() as es:
        C = Ctx(nc, es)
        P = Prog(nc)
        xT = C.din("xT", [128, KC, NTK])
        yT = C.din("yT", [128, 4 * 8, NTK], BF16)
        mods = C.din("mods", [128, 8, KC])
        wg = C.din("wg", [4, KC, 128, KC, 128])
        wb = C.din("wb", [4, KC, 128, 8, 128])
        wo = C.din("wo", [KC, 128, KC, 128])
        onesd = C.din("onesf", [128, 128])
        out = C.dout("out", [128, KC, NTK])
        K = load_consts(P, C, {"onesf": onesd}, ["onesf"], [])
        md = C.sb([128, 8, KC], F32)
        P.dma(md[:], mods, (), [md])
        sc1 = C.sb([128, 2, KC], F32)
        P.ts("dve", sc1[:, 0, :], md[:, 0, :], 1.0, None, ALU.add, None, [md], [sc1])
        P.ts("dve", sc1[:, 1, :], md[:, 3, :], 1.0, None, ALU.add, None, [md], [sc1])

        xf = C.sb([128, KC, TP], F32, "xf")
        sq = C.sb([128, KC, TP], F32, "sq")
        hT = C.sb([128, KC, TP], BF16, "hT")
        yt = C.sb([128, 32, TP], BF16, "yt")
        mT = C.sb([128, KC, TP], BF16, "mT")
        sT = C.sb([128, KC, TP], F32, "sT")
        ot = xf
        st = [C.sb([128, TP], F32, "st%d" % i) for i in range(4)]
        wgr = Ring([C.sb([128, KC, 128], BF16, "wg") for _ in range(4)])
        wbr = Ring([C.sb([128, 8, 128], BF16, "wb") for _ in range(4)])
        sig = Ring([C.sb([128, TP], F32, "sig") for _ in range(2)])
        acc = C.sb([128, TP], F32, "acc")
        tmp = Ring([C.sb([128, TP], F32, "tmp") for _ in range(2)])
        psr = Ring([C.ps([128, 512], F32) for _ in range(6)])
        odr = Buf(None)

        passes = []
        if NCT:
            passes.append((0, NCT, 1))
        t = NCT
        while t < NTK:
            n = min(TP, NTK - t)
            passes.append((t, n, 0))
            t += n
        for (t0, T, isctx) in passes:
            P.dma(xf[:, :, 0:T], xT[:, :, t0:t0 + T], (), [xf])
            P.dma(yt[:, :, 0:T], yT[:, :, t0:t0 + T], (), [yt])
            emit_ln_fm(P, C, K, xf, T, KC, sc1[:, isctx, :], md[:, 1 + 3 * isctx, :], hT,
                       lambda kc: hT[:, kc, 0:T], sq, psr, st)
            gate_ap = md[:, 2 + 3 * isctx, :]
            for n in range(KC):
                for i in range(4):
                    wgt = wgr.next()
                    P.dma(wgt[:], wg[i, n], (), [wgt], q="pool")
                    wbt = wbr.next()
                    P.dma(wbt[:], wb[i, n], (), [wbt], q="pool")
                    pg = psr.next()
                    for kc in range(KC):
                        P.mm(pg[:, 0:T], wgt[:, kc, :], hT[:, kc, 0:T], kc == 0, kc == KC - 1, [wgt, hT], [pg])
                    pb = psr.next()
                    for kc in range(8):
                        P.mm(pb[:, 0:T], wbt[:, kc, :], yt[:, i * 8 + kc, 0:T], kc == 0, kc == 7, [wbt, yt], [pb])
                    sg = sig.next()
                    P.act(sg[:, 0:T], pg[:, 0:T], AF.Sigmoid, [pg], [sg])
                    if i == 0:
                        P.tt("dve", acc[:, 0:T], sg[:, 0:T], pb[:, 0:T], ALU.mult, [sg, pb], [acc])
                    else:
                        tm = tmp.next()
                        P.tt("dve", tm[:, 0:T], sg[:, 0:T], pb[:, 0:T], ALU.mult, [sg, pb], [tm])
                        if i < 3:
                            P.tt("dve", acc[:, 0:T], acc[:, 0:T], tm[:, 0:T], ALU.add, [acc, tm], [acc])
                        else:
                            P.tt("dve", mT[:, n, 0:T], acc[:, 0:T], tm[:, 0:T], ALU.add, [acc, tm], [mT])
            P.dma(xf[:, :, 0:T], xT[:, :, t0:t0 + T], (), [xf])
            for n in range(KC):
                wot = wgr.next()
                P.dma(wot[:], wo[n], (), [wot], q="pool")
                po = psr.next()
                for kc in range(KC):
                    P.mm(po[:, 0:T], wot[:, kc, :], mT[:, kc, 0:T], kc == 0, kc == KC - 1, [wot, mT], [po])
                tm = tmp.next()
                P.ts("dve", tm[:, 0:T], po[:, 0:T], gate_ap[:, n:n + 1], None, ALU.mult, None, [po, md], [tm])
                P.stt("dve", sT[:, n, 0:T], xf[:, n, 0:T], ALPHA, tm[:, 0:T], ALU.mult, ALU.add, [xf, tm], [sT])
            emit_ln_fm(P, C, K, sT, T, KC, md[:, 6, :], md[:, 7, :], ot,
                       lambda kc: ot[:, kc, 0:T], sq, psr, st)
            P.dma(out[:, :, t0:t0 + T], ot[:, :, 0:T], [ot], [odr], q="pool")
        P.finish([odr])
        P.close()
    return nc


CONV_GROUPS = (0, 1, 3)
(CI_ID, CI_ONES, CI_U, CI_UT, CI_NEGF, CI_NEGB, CI_G1F, CI_G2F, CI_G3F, CI_G1B, CI_G2B, CI_G3B,
 CI_LAGF, CI_LAGB, CI_N) = range(15)


def host_constf():
    j = np.arange(128)[:, None].astype(np.float64)
    i = np.arange(128)[None, :].astype(np.float64)
    U = (j <= i).astype(np.float64)
    UT = (j >= i).astype(np.float64)
    c = np.zeros((CI_N, 128, 128))
    c[CI_ID] = np.eye(128)
    c[CI_ONES] = 1.0
    c[CI_U] = U
    c[CI_UT] = UT
    c[CI_NEGF] = (U - 1.0) * 30000.0
    c[CI_NEGB] = (UT - 1.0) * 30000.0
    s = -1.0 / 16.0
    c[CI_G1F] = s * (U - U[:, 64:65])
    c[CI_G2F] = s * U
    c[CI_G3F] = s * (1.0 - U)
    c[CI_G1B] = s * (UT - UT[:, 63:64])
    c[CI_G2B] = s * UT
    c[CI_G3B] = s * (1.0 - UT)
    c[CI_LAGF] = np.maximum(i - j, 0.0)
    c[CI_LAGB] = np.maximum(j - i, 0.0)
    return np.ascontiguousarray(c.transpose(1, 0, 2)).astype(np.float32)


def host_constb():
    c = np.zeros((128, 2, 128), np.float32)
    c[:, 0, :] = np.eye(128)
    c[:, 1, :] = np.eye(128)[::-1]
    return c.astype(NPBF)


class Sync:
    @staticmethod
    def barrier(P):
        deps = {k: v for k, v in P.cnt.items() if v > 0}
        for e in P.eng:
            P._wait(e, dict(deps))


def build_mix(KC, LAT, stages=("ln", "wprep", "proj", "hy", "ssd", "gla", "ret")):
    D = 128 * KC
    NTL = LAT // 128
    NTOK = CTX + LAT
    NTP = NTOK + 4
    NTA = 2 + NTL
    nc = bass.Bass("TRN2", target_bir_lowering=False)
    with ExitStack() as es:
        C = Ctx(nc, es)
        P = Prog(nc)
        xT = C.din("xT", [128, KC, NTOK])
        mods = C.din("mods", [128, 4, KC])
        wc = C.din("wc", [8, 128, KC, 512])
        cw = C.din("cw", [3, 3, 512])
        cb = C.din("cb", [1, 3, 512])
        constf_d = C.din("constf", [128, CI_N, 128])
        constb_d = C.din("constb", [128, 2, 128], BF16)
        hy_w1 = C.din("hy_w1", [33, 64]); hy_w2 = C.din("hy_w2", [64, 64]); hy_w3 = C.din("hy_w3", [64, 64])
        hy_bf = C.din("hy_bf", [64, 6])
        hy_w4 = C.din("hy_w4", [64, 4, 256])
        hy_skip = C.din("hy_skip", [1, 2, 256])
        hy_negd = C.din("hy_negd", [128, 2])
        zx = C.din("zx", [2, 33, LAT]); zc = C.din("zc", [2, 33, CTX])
        tx = C.din("tx", [2, LAT]); tcx = C.din("tcx", [2, CTX])
        ssd_pr = C.din("ssd_pr", [1, 4, 8])
        ssd_row = C.din("ssd_row", [1, 2, 256])
        gla_w2 = C.din("gla_w2", [16, 2, 128]); gla_b2 = C.din("gla_b2", [1, 2, 128])
        gla_nw = C.din("gla_nw", [1, 256])
        ret_dec = C.din("ret_dec", [1, 2])
        ret_pos = C.din("ret_pos", [128, 4])
        ret_cs = C.din("ret_cs", [LAT, 2, 256])
        Y = C.dout("Y", [NTOK, 1024], BF16)
        hTd = C.dscr("hTd", [128, KC, NTP], BF16)
        wbd = C.dscr("wbd", [14, 128, KC, 512], BF16)
        ptm = C.dscr("ptm", [NTOK, 4096], F32)
        kxx = C.dscr("kxx", [512, 2 * LAT], BF16)
        kxc = C.dscr("kxc", [512, 2 * CTX], BF16)
        d_hT, d_wb, d_ptm, d_kx, d_Y = Buf(None), Buf(None), Buf(None), Buf(None), Buf(None)

        cf = C.sb([128, CI_N, 128], F32, "cf")
        P.dma(cf[:], constf_d, (), [cf])
        cbf = C.sb([128, 2, 128], BF16, "cbf")
        P.dma(cbf[:], constb_d, (), [cbf])
        K = {"onesf": cf}

        class _Ones:
            writers = cf.writers
            readers = cf.readers

            def __getitem__(self, idx):
                return cf[:, CI_ONES, :]
        K["onesf"] = _Ones()
        PSF = Ring([C.ps([128, 512], F32, "psf") for _ in range(6)])
        PSB = Ring([C.ps([128, 512], BF16, "psb") for _ in range(2)])

        def cm(i):
            return cf[:, i, :]

        blk_of = {}
        nb = 0
        for g in range(8):
            for tap in (range(3) if g in CONV_GROUPS else range(1)):
                blk_of[(g, tap)] = nb
                nb += 1
        assert nb == 14
        tiles = [(0, 1, 1, -1), (128, 129, 1, -1)] + [(256 + 128 * i, 259 + 128 * i, 0, i) for i in range(NTL)]

        if "ln" in stages:
            with ExitStack() as es2:
                C2 = Ctx(nc, es2)
                md = C2.sb([128, 4, KC], F32)
                P.dma(md[:], mods, (), [md])
                sc1 = C2.sb([128, 2, KC], F32)
                P.ts("dve", sc1[:, 0, :], md[:, 0, :], 1.0, None, ALU.add, None, [md], [sc1])
                P.ts("dve", sc1[:, 1, :], md[:, 2, :], 1.0, None, ALU.add, None, [md], [sc1])
                zt = C2.sb([128, KC, 2], BF16)
                P.memset("dve", zt, zt[:], 0.0)
                for pos in (0, 257, NTP - 1):
                    w_ = 2 if pos == 257 else 1
                    P.dma(hTd[:, :, pos:pos + w_], zt[:, :, 0:w_], [zt], [d_hT], q="pool", allow_slow_non_contiguous=True)
                TB = 256
                xfr = Ring([C2.sb([128, KC, TB], F32, "xf") for _ in range(2)])
                sq = C2.sb([128, KC, TB], F32, "sq")
                hbr = Ring([C2.sb([128, KC, TB], BF16, "hb") for _ in range(2)])
                st = [C2.sb([128, TB], F32, "st%d" % i) for i in range(4)]
                for t0 in range(0, NTOK, TB):
                    isctx = 1 if t0 < CTX else 0
                    xf = xfr.next()
                    hb = hbr.next()
                    P.dma(xf[:], xT[:, :, t0:t0 + TB], (), [xf])
                    emit_ln_fm(P, C2, K, xf, TB, KC, sc1[:, isctx, :], md[:, 1 + 2 * isctx, :], hb,
                               lambda kc: hb[:, kc, :], sq, PSF, st)
                    pos = t0 + 1 if isctx else t0 + 3
                    P.dma(hTd[:, :, pos:pos + TB], hb[:], [hb], [d_hT], q="pool")
                Sync.barrier(P)

        if "wprep" in stages:
            with ExitStack() as es2:
                C2 = Ctx(nc, es2)
                kk = min(8, KC)
                cwt = C2.sb([128, 3, 3, 512], F32, "cwt")
                for tap in range(3):
                    for gi in range(3):
                        P.dma(cwt[:, tap, gi, :], cw[tap, gi:gi + 1, :].to_broadcast([128, 512]), (), [cwt])
                wfr = Ring([C2.sb([128, kk, 512], F32, "wf") for _ in range(2)])
                wbr = Ring([C2.sb([128, kk, 512], BF16, "wb") for _ in range(3)])
                engs = ["dve", "pool"]
                ne = 0
                for g in range(8):
                    for kc0 in range(0, KC, kk):
                        wf = wfr.next()
                        P.dma(wf[:], wc[g, :, kc0:kc0 + kk, :], (), [wf])
                        if g in CONV_GROUPS:
                            gi = CONV_GROUPS.index(g)
                            for tap in range(3):
                                wb = wbr.next()
                                e = engs[ne % 2]
                                ne += 1
                                P.tt(e, wb[:], wf[:], bc_mid(cwt[:, tap, gi, :], kk), ALU.mult, [wf, cwt], [wb])
                                P.dma(wbd[blk_of[(g, tap)], :, kc0:kc0 + kk, :], wb[:], [wb], [d_wb], q="pool")
                        else:
                            wb = wbr.next()
                            e = engs[ne % 2]
                            ne += 1
                            P.copy(e, wb[:], wf[:], [wf], [wb])
                            P.dma(wbd[blk_of[(g, 0)], :, kc0:kc0 + kk, :], wb[:], [wb], [d_wb], q="pool")
                Sync.barrier(P)

        if "proj" in stages:
            with ExitStack() as es2:
                C2 = Ctx(nc, es2)
                cbt = C2.sb([128, 3, 512], F32, "cbt")
                for gi in range(3):
                    P.dma(cbt[:, gi, :], cb[0:1, gi, :].to_broadcast([128, 512]), (), [cbt])
                TBT = 4
                hsr = Ring([C2.sb([128, KC, TBT * 128 + 2], BF16, "hs") for _ in range(2)])
                wr = Ring([C2.sb([128, KC, 512], BF16, "w") for _ in range(3)])
                otr = Ring([C2.sb([128, 512], F32, "ot") for _ in range(3)])
                sbs = [tiles[0:2]] + [tiles[2 + i:2 + i + TBT] for i in range(0, NTL, TBT)]
                for sbt in sbs:
                    s0 = sbt[0][1]
                    nt = len(sbt)
                    hs = hsr.next()
                    P.dma(hs[:, :, 0:nt * 128 + 2], hTd[:, :, s0 - 1:s0 + nt * 128 + 1], [d_hT], [hs])
                    for g in range(8):
                        taps = list(range(3)) if g in CONV_GROUPS else [1]
                        pacc = [PSF.next() for _ in range(nt)]
                        for ti_, tap in enumerate(taps):
                            wt = wr.next()
                            P.dma(wt[:], wbd[blk_of[(g, tap if g in CONV_GROUPS else 0)]], [d_wb], [wt])
                            for ti in range(nt):
                                for kc in range(KC):
                                    P.mm(pacc[ti][:, :], hs[:, kc, ti * 128 + tap:ti * 128 + tap + 128], wt[:, kc, :],
                                         ti_ == 0 and kc == 0, ti_ == len(taps) - 1 and kc == KC - 1,
                                         [hs, wt], [pacc[ti]])
                        for ti in range(nt):
                            ot = otr.next()
                            if g in CONV_GROUPS:
                                gi = CONV_GROUPS.index(g)
                                P.tt("dve", ot[:], pacc[ti][:, :], cbt[:, gi, :], ALU.add, [pacc[ti], cbt], [ot])
                                if g == 3:
                                    silu_inplace(P, C2, ot, ot[:], otr.next())
                            else:
                                P.copy("act", ot[:], pacc[ti][:, :], [pacc[ti]], [ot])
                            tok0 = sbt[ti][0]
                            P.dma(ptm[tok0:tok0 + 128, g * 512:(g + 1) * 512], ot[:], [ot], [d_ptm], q="pool")
                Sync.barrier(P)

        env = dict(nc=nc, P=P, cf=cf, cbf=cbf, cm=cm, PSF=PSF, PSB=PSB, ptm=ptm, Y=Y, d_ptm=d_ptm, d_Y=d_Y,
                   tiles=tiles, NTL=NTL, NTA=NTA, LAT=LAT, KC=KC)
        if "ret" in stages:
            mix_ret(env, ret_dec, ret_pos, ret_cs)
            Sync.barrier(P)
        if "gla" in stages:
            mix_gla(env, gla_w2, gla_b2, gla_nw)
            Sync.barrier(P)
        if "ssd" in stages:
            mix_ssd(env, ssd_pr, ssd_row)
            Sync.barrier(P)
        if "hy" in stages:
            mix_hyena(env, hy_w1, hy_w2, hy_w3, hy_bf, hy_w4, hy_skip, hy_negd, zx, zc, tx, tcx, kxx, kxc, d_kx)
            Sync.barrier(P)
        P.finish([d_Y])
        print("mix instructions", P.n_inst)
        P.close()
    return nc


def silu_inplace(P, C, buf, ap, tmpbuf):
    shp = list(ap.shape)
    t = tmpbuf
    tv = t[:, 0:shp[1]] if len(shp) == 2 else t[:]
    P.act(tv, ap, AF.Sigmoid, [buf], [t])
    P.tt("dve", ap, ap, tv, ALU.mult, [buf, t], [buf])


def row_rsqrt_mean(P, W, y_buf, y_ap, F_, rstd_buf):
    nc = P.nc
    junk = W["junk"]
    ss = W["ss"]
    P.act(junk[:, 0:F_], y_ap, AF.Square, [y_buf], [junk])
    P.op("dve", lambda: nc.vector.reduce_sum(out=ss[:, 0:1], in_=junk[:, 0:F_], axis=AX.X), [junk], [ss])
    emit_rsqrt(P, rstd_buf, rstd_buf[:, 0:1], ss, ss[:, 0:1], EPS, 1.0 / F_)


def cla_step(env, W, QdT, KdT, QsT, ndk, Vb, V_ap, Keb, Ke_ap, maskb, mask_ap, rowscale, dec_ap, decb,
             S, Sb, s_sl, dv, ybuf, y_ap, first):
    P, PSF = env["P"], env["PSF"]
    pa = PSF.next()
    for kc in range(ndk):
        P.mm(pa[:, 0:128], KdT[1](kc), QdT[1](kc), kc == 0, kc == ndk - 1, [KdT[0], QdT[0]], [pa])
    am = W["am"].next()
    P.tt("dve", am[:], pa[:, 0:128], mask_ap, ALU.mult, [pa, maskb], [am])
    py1 = PSF.next()
    P.mm(py1[:, 0:dv], am[:], V_ap, True, True, [am, Vb], [py1])
    py2 = PSF.next()
    for kc in range(ndk):
        P.mm(py2[:, 0:dv], QsT[1](kc), Sb[:, kc, s_sl], kc == 0, kc == ndk - 1, [QsT[0], Sb], [py2])
    y1 = W["y1"].next()
    P.copy("act", y1[:, 0:dv], py1[:, 0:dv], [py1], [y1])
    if first:
        tgt, tb = y_ap, ybuf
    else:
        tb = W["y2"].next()
        tgt = tb[:, 0:dv]
    if rowscale is not None:
        P.stt("dve", tgt, py2[:, 0:dv], rowscale[1], y1[:, 0:dv], ALU.mult, ALU.add, [py2, y1, rowscale[0]], [tb])
    else:
        P.tt("dve", tgt, py2[:, 0:dv], y1[:, 0:dv], ALU.add, [py2, y1], [tb])
    if not first:
        P.tt("pool", y_ap, y_ap, tgt, ALU.add, [ybuf, tb], [ybuf])
    for mc in range(ndk):
        ps = PSF.next()
        P.mm(ps[:, 0:dv], Ke_ap(mc), V_ap, True, True, [Keb, Vb], [ps])
        P.stt("dve", S[:, mc, s_sl], S[:, mc, s_sl], dec_ap(mc), ps[:, 0:dv], ALU.mult, ALU.add, [S, decb, ps], [S])
        P.copy("act", Sb[:, mc, s_sl], S[:, mc, s_sl], [S], [Sb])


def tile_order(env, d):
    t = env["tiles"]
    if d == 0:
        return list(range(len(t)))
    return [1, 0] + list(range(len(t) - 1, 1, -1))


def transpose_bf(env, W, src_buf, src_ap, dst_buf, dst_ap):
    P, PSB, cbf = env["P"], env["PSB"], env["cbf"]
    pt = PSB.next()
    P.tr(pt[:, 0:128], src_ap, cbf[:, 0, :], [src_buf, cbf], [pt])
    P.copy("act", dst_ap, pt[:, 0:128], [pt], [dst_buf])


def _mk_work(C2, dv):
    return {"am": Ring([C2.sb([128, 128], BF16, "am") for _ in range(2)]),
            "y1": Ring([C2.sb([128, 256], F32, "y1") for _ in range(2)]),
            "y2": Ring([C2.sb([128, 256], F32, "y2") for _ in range(2)]),
            "junk": C2.sb([128, 256], F32, "junk"), "ss": C2.sb([128, 1], F32, "ss")}


def finish_tile(env, W, C2, ybuf, ti, gate_col, out_col, kind, roww=None, pre=None):
    P, nc, ptm, Y = env["P"], env["nc"], env["ptm"], env["Y"]
    tok0 = env["tiles"][ti][0]
    g = W["g"].next()
    P.dma(g[:], ptm[tok0:tok0 + 128, gate_col:gate_col + 256], [env["d_ptm"]], [g])
    y = ybuf[:, ti, :]
    if pre is not None:
        pre(y, tok0)
    if kind == "ln":
        m = W["m"]
        P.op("dve", lambda: nc.vector.reduce_sum(out=m[:, 0:1], in_=y, axis=AX.X), [ybuf], [m])
        P.ts("dve", m[:, 0:1], m[:, 0:1], -1.0 / 256, None, ALU.mult, None, [m], [m])
        P.ts("dve", y, y, m[:, 0:1], None, ALU.add, None, [ybuf, m], [ybuf])
    rs = W["rs"]
    row_rsqrt_mean(P, W, ybuf, y, 256, rs)
    sg = W["sg"].next()
    P.act(sg[:], g[:], AF.Sigmoid, [g], [sg])
    P.tt("pool", g[:], g[:], sg[:], ALU.mult, [g, sg], [g])
    if roww is not None:
        P.tt("pool", g[:], g[:], roww[1], ALU.mult, [g, roww[0]], [g])
    ob = W["ob"].next()
    P.stt("dve", ob[:], y, rs[:, 0:1], g[:], ALU.mult, ALU.mult, [ybuf, rs, g], [ob])
    P.dma(Y[tok0:tok0 + 128, out_col:out_col + 256], ob[:], [ob], [env["d_Y"]], q="pool")


def _fin_work(C2, W):
    W["g"] = Ring([C2.sb([128, 256], F32, "g") for _ in range(2)])
    W["sg"] = Ring([C2.sb([128, 256], F32, "sg") for _ in range(2)])
    W["ob"] = Ring([C2.sb([128, 256], BF16, "ob") for _ in range(2)])
    W["m"] = C2.sb([128, 1], F32, "m")
    W["rs"] = C2.sb([128, 1], F32, "rs")


def mix_ret(env, ret_dec, ret_pos, ret_cs):
    nc, P, cf, cm, ptm = env["nc"], env["P"], env["cf"], env["cm"], env["ptm"]
    NTA, tiles = env["NTA"], env["tiles"]
    with ExitStack() as es2:
        C2 = Ctx(nc, es2)
        W = _mk_work(C2, 256)
        _fin_work(C2, W)
        ybuf = C2.sb([128, NTA, 256], F32, "ybuf")
        lam = C2.sb([128, 2], F32, "lam")
        P.dma(lam[:], ret_dec.to_broadcast([128, 2]), (), [lam])
        P.act(lam[:], lam[:], AF.Exp, [lam], [lam])
        P.ts("dve", lam[:], lam[:], -1.0, None, ALU.mult, None, [lam], [lam])
        pos = C2.sb([128, 4], F32, "pos")
        P.dma(pos[:], ret_pos, (), [pos])
        fs = C2.sb([128, 2], F32, "fs")
        te = C2.sb([128, 2], F32, "te")
        dc = C2.sb([128, 2], F32, "dc")
        mk = C2.sb([128, 2, 128], F32, "mk")
        for d in range(2):
            P.act(fs[:, d:d + 1], pos[:, 2 * d:2 * d + 1], AF.Exp, [pos, lam], [fs], scale=lam[:, d:d + 1])
            P.act(te[:, d:d + 1], pos[:, 2 * d + 1:2 * d + 2], AF.Exp, [pos, lam], [te], scale=lam[:, d:d + 1])
            P.act(dc[:, d:d + 1], lam[:, d:d + 1], AF.Exp, [lam], [dc], scale=128.0)
            P.act(mk[:, d, :], cm(CI_LAGF + d), AF.Exp, [cf, lam], [mk], scale=lam[:, d:d + 1])
            P.tt("dve", mk[:, d, :], mk[:, d, :], cm(CI_U + d), ALU.mult, [mk, cf], [mk])
        P.ts("dve", te[:], te[:], 1.0 / 16, None, ALU.mult, None, [te], [te])
        S = C2.sb([128, 2, 256], F32, "S")
        Sb = C2.sb([128, 2, 256], BF16, "Sb")
        rawr = Ring([C2.sb([128, 768], F32, "raw") for _ in range(2)])
        csr = Ring([C2.sb([128, 2, 256], F32, "cs") for _ in range(2)])
        rot = Ring([C2.sb([128, 2, 256], F32, "rot") for _ in range(2)])
        tmpr = Ring([C2.sb([128, 256], F32, "tmp") for _ in range(2)])
        qbr = Ring([C2.sb([128, 4, 256], BF16, "qb") for _ in range(2)])
        qtr = Ring([C2.sb([128, 4, 128], BF16, "qt") for _ in range(2)])
        for d in range(2):
            P.memset("dve", S, S[:], 0.0)
            P.memset("dve", Sb, Sb[:], 0.0)
            for ti in tile_order(env, d):
                tok0, _, isctx, lt = tiles[ti]
                raw = rawr.next()
                P.dma(raw[:], ptm[tok0:tok0 + 128, 3072:3840], [env["d_ptm"]], [raw])
                qb = qbr.next()
                if isctx:
                    qsrc, ksrc, qkb = raw[:, 0:256], raw[:, 256:512], raw
                else:
                    cs = csr.next()
                    P.dma(cs[:], ret_cs[lt * 128:(lt + 1) * 128], (), [cs])
                    ro = rot.next()
                    for w_ in range(2):
                        x = raw[:, w_ * 256:(w_ + 1) * 256]
                        tm = tmpr.next()
                        for g in range(2):
                            b0 = g * 128
                            P.tt("pool", tm[:, b0:b0 + 64], raw[:, w_ * 256 + b0 + 64:w_ * 256 + b0 + 128],
                                 cs[:, 1, b0:b0 + 64], ALU.mult, [raw, cs], [tm])
                            P.tt("pool", tm[:, b0 + 64:b0 + 128], raw[:, w_ * 256 + b0:w_ * 256 + b0 + 64],
                                 cs[:, 1, b0 + 64:b0 + 128], ALU.mult, [raw, cs], [tm])
                        P.tt("dve", ro[:, w_, :], x, cs[:, 0, :], ALU.mult, [raw, cs], [ro])
                        P.tt("dve", ro[:, w_, :], ro[:, w_, :], tm[:], ALU.add, [ro, tm], [ro])
                    qsrc, ksrc, qkb = ro[:, 0, :], ro[:, 1, :], ro
                P.copy("act", qb[:, 0, :], qsrc, [qkb], [qb])
                P.ts("dve", qb[:, 1, :], ksrc, 1.0 / 16, None, ALU.mult, None, [qkb], [qb])
                P.copy("act", qb[:, 2, :], raw[:, 512:768], [raw], [qb])
                P.ts("dve", qb[:, 3, :], ksrc, te[:, d:d + 1], None, ALU.mult, None, [qkb, te], [qb])
                qt = qtr.next()
                for w_ in range(2):
                    for kc in range(2):
                        transpose_bf(env, W, qb, qb[:, w_, kc * 128:(kc + 1) * 128], qt, qt[:, w_ * 2 + kc, :])
                cla_step(env, W, (qt, lambda kc: qt[:, kc, :]), (qt, lambda kc: qt[:, 2 + kc, :]),
                         (qt, lambda kc: qt[:, kc, :]), 2, qb, qb[:, 2, :], qb,
                         lambda mc: qb[:, 3, mc * 128:(mc + 1) * 128], mk, mk[:, d, :],
                         (fs, fs[:, d:d + 1]), lambda mc: dc[:, d:d + 1], dc, S, Sb, slice(0, 256), 256,
                         ybuf, ybuf[:, ti, :], d == 0)
        for ti in range(NTA):
            finish_tile(env, W, C2, ybuf, ti, 3840, 768, "ln")


def mix_gla(env, gla_w2, gla_b2, gla_nw):
    nc, P, cf, cm, ptm, PSF = env["nc"], env["P"], env["cf"], env["cm"], env["ptm"], env["PSF"]
    NTA, tiles = env["NTA"], env["tiles"]
    with ExitStack() as es2:
        C2 = Ctx(nc, es2)
        W = _mk_work(C2, 256)
        _fin_work(C2, W)
        ybuf = C2.sb([128, NTA, 256], F32, "ybuf")
        w2 = C2.sb([16, 2, 128], F32, "w2")
        P.dma(w2[:], gla_w2, (), [w2])
        b2 = C2.sb([128, 2, 128], F32, "b2")
        P.dma(b2[:], gla_b2.to_broadcast([128, 2, 128]), (), [b2])
        nw = C2.sb([128, 256], F32, "nw")
        P.dma(nw[:], gla_nw.to_broadcast([128, 256]), (), [nw])
        onec = C2.sb([128, 1], F32, "onec")
        P.memset("dve", onec, onec[:], 1.0)
        S = C2.sb([128, 1, 256], F32, "S")
        Sb = C2.sb([128, 1, 256], BF16, "Sb")
        rawr = Ring([C2.sb([128, 512], F32, "raw") for _ in range(2)])
        lrr = Ring([C2.sb([128, 32], F32, "lr") for _ in range(2)])
        lrt = Ring([C2.sb([16, 128], F32, "lrt") for _ in range(2)])
        spr = Ring([C2.sb([128, 128], F32, "sp") for _ in range(2)])
        exr = Ring([C2.sb([128, 4, 128], F32, "ex") for _ in range(2)])
        opr = Ring([C2.sb([128, 4, 128], BF16, "op") for _ in range(2)])
        otr = Ring([C2.sb([128, 3, 128], BF16, "opt") for _ in range(2)])
        vbr = Ring([C2.sb([128, 256], BF16, "vb") for _ in range(2)])
        dcr = Ring([C2.sb([128, 1], F32, "dc") for _ in range(2)])
        qsc = 128.0 ** -0.5
        for d in range(2):
            P.memset("dve", S, S[:], 0.0)
            P.memset("dve", Sb, Sb[:], 0.0)
            for ti in tile_order(env, d):
                tok0 = tiles[ti][0]
                raw = rawr.next()
                P.dma(raw[:], ptm[tok0:tok0 + 128, 2048:2560], [env["d_ptm"]], [raw])
                lr = lrr.next()
                P.dma(lr[:], ptm[tok0:tok0 + 128, 2560 + 264:2560 + 296], [env["d_ptm"]], [lr])
                pt = PSF.next()
                P.tr(pt[0:16, 0:128], lr[:, d * 16:(d + 1) * 16], cm(CI_ID), [lr, cf], [pt])
                lt_ = lrt.next()
                P.copy("act", lt_[:], pt[0:16, 0:128], [pt], [lt_])
                pl = PSF.next()
                P.mm(pl[:, 0:128], lt_[:], w2[:, d, :], True, True, [lt_, w2], [pl])
                sp = spr.next()
                P.tt("dve", sp[:], pl[:, 0:128], b2[:, d, :], ALU.add, [pl, b2], [sp])
                P.act(sp[:], sp[:], AF.Exp, [sp], [sp], scale=-1.0)
                P.act(sp[:], sp[:], AF.Ln, [sp], [sp], bias=onec[:, 0:1], scale=1.0)
                ex = exr.next()
                base = CI_G1F + 3 * d
                pg = [PSF.next() for _ in range(3)]
                for k_ in range(3):
                    P.mm(pg[k_][:, 0:128], cm(base + k_), sp[:], True, True, [cf, sp], [pg[k_]])
                P.act(ex[:, 0, :], pg[0][:, 0:128], AF.Exp, [pg[0]], [ex])
                P.act(ex[:, 1, :], pg[0][:, 0:128], AF.Exp, [pg[0]], [ex], scale=-1.0)
                P.act(ex[:, 2, :], pg[1][:, 0:128], AF.Exp, [pg[1]], [ex])
                P.act(ex[:, 3, :], pg[2][:, 0:128], AF.Exp, [pg[2]], [ex])
                pd = PSF.next()
                P.mm(pd[:, 0:1], sp[:], onec[:, 0:1], True, True, [sp, onec], [pd])
                dc = dcr.next()
                P.act(dc[:], pd[:, 0:1], AF.Exp, [pd], [dc], scale=-1.0 / 16)
                op = opr.next()
                q, k = raw[:, 0:128], raw[:, 128:256]
                P.stt("dve", op[:, 0, :], q, qsc, ex[:, 0, :], ALU.mult, ALU.mult, [raw, ex], [op])
                P.tt("pool", op[:, 1, :], k, ex[:, 1, :], ALU.mult, [raw, ex], [op])
                P.stt("dve", op[:, 2, :], q, qsc, ex[:, 2, :], ALU.mult, ALU.mult, [raw, ex], [op])
                P.tt("pool", op[:, 3, :], k, ex[:, 3, :], ALU.mult, [raw, ex], [op])
                vb = vbr.next()
                P.copy("act", vb[:], raw[:, 256:512], [raw], [vb])
                ot = otr.next()
                for k_ in range(3):
                    transpose_bf(env, W, op, op[:, k_, :], ot, ot[:, k_, :])
                cla_step(env, W, (ot, lambda kc: ot[:, 0, :]), (ot, lambda kc: ot[:, 1, :]),
                         (ot, lambda kc: ot[:, 2, :]), 1, vb, vb[:], op, lambda mc: op[:, 3, :],
                         cf, cm(CI_U + d), None, lambda mc: dc[:, 0:1], dc, S, Sb, slice(0, 256), 256,
                         ybuf, ybuf[:, ti, :], d == 0)
        for ti in range(NTA):
            finish_tile(env, W, C2, ybuf, ti, 2560, 512, "rms", roww=(nw, nw[:]))


def mix_ssd(env, ssd_pr, ssd_row):
    nc, P, cf, cm, ptm, PSF = env["nc"], env["P"], env["cf"], env["cm"], env["ptm"], env["PSF"]
    NTA, tiles = env["NTA"], env["tiles"]
    with ExitStack() as es2:
        C2 = Ctx(nc, es2)
        W = _mk_work(C2, 64)
        _fin_work(C2, W)
        ybuf = C2.sb([128, NTA, 256], F32, "ybuf")
        pr = C2.sb([128, 4, 8], F32, "pr")
        P.dma(pr[:], ssd_pr.to_broadcast([128, 4, 8]), (), [pr])
        P.act(pr[:, 0, :], pr[:, 0, :], AF.Exp, [pr], [pr])
        P.ts("dve", pr[:, 0, :], pr[:, 0, :], -1.0, None, ALU.mult, None, [pr], [pr])
        rw = C2.sb([128, 2, 256], F32, "rw")
        P.dma(rw[:], ssd_row.to_broadcast([128, 2, 256]), (), [rw])
        onec = C2.sb([128, 1], F32, "onec")
        P.memset("dve", onec, onec[:], 1.0)
        S = C2.sb([128, 1, 256], F32, "S")
        Sb = C2.sb([128, 1, 256], BF16, "Sb")
        rawr = Ring([C2.sb([128, 512], F32, "raw") for _ in range(2)])
        dtr = Ring([C2.sb([128, 3, 8], F32, "dt") for _ in range(2)])
        bcb = Ring([C2.sb([128, 3, 128], BF16, "bcb") for _ in range(2)])
        bct = Ring([C2.sb([128, 2, 128], BF16, "bct") for _ in range(2)])
        scr = Ring([C2.sb([128, 128], F32, "sc") for _ in range(2)])
        acr = Ring([C2.sb([128, 4, 4], F32, "ac") for _ in range(2)])
        abr = Ring([C2.sb([128, 128], F32, "ab") for _ in range(3)])
        mkr = Ring([C2.sb([128, 128], F32, "mk") for _ in range(3)])
        alr = Ring([C2.sb([128, 2], F32, "al") for _ in range(4)])
        vbr = Ring([C2.sb([128, 256], BF16, "vb") for _ in range(2)])
        ker = Ring([C2.sb([128, 128], BF16, "ke") for _ in range(3)])
        for d in range(2):
            P.memset("dve", S, S[:], 0.0)
            P.memset("dve", Sb, Sb[:], 0.0)
            Ud = CI_U + d
            endcol = 127 if d == 0 else 0
            for ti in tile_order(env, d):
                tok0 = tiles[ti][0]
                raw = rawr.next()
                P.dma(raw[:], ptm[tok0:tok0 + 128, 1536:2048], [env["d_ptm"]], [raw])
                dt = dtr.next()
                P.dma(dt[:, 0, :], ptm[tok0:tok0 + 128, 2560 + 256:2560 + 264], [env["d_ptm"]], [dt])
                P.tt("dve", dt[:, 0, :], dt[:, 0, :], pr[:, 1, :], ALU.add, [dt, pr], [dt])
                P.act(dt[:, 0, :], dt[:, 0, :], AF.Exp, [dt], [dt])
                P.act(dt[:, 0, :], dt[:, 0, :], AF.Ln, [dt], [dt], bias=onec[:, 0:1], scale=1.0)
                P.tt("dve", dt[:, 1, :], dt[:, 0, :], pr[:, 0, :], ALU.mult, [dt, pr], [dt])
                bc = bcb.next()
                P.copy("act", bc[:, 0, :], raw[:, 256:384], [raw], [bc])
                P.copy("act", bc[:, 1, :], raw[:, 384:512], [raw], [bc])
                bt = bct.next()
                transpose_bf(env, W, bc, bc[:, 0, :], bt, bt[:, 0, :])
                transpose_bf(env, W, bc, bc[:, 1, :], bt, bt[:, 1, :])
                pa = PSF.next()
                P.mm(pa[:, 0:128], bt[:, 0, :], bt[:, 1, :], True, True, [bt], [pa])
                sc = scr.next()
                P.copy("act", sc[:], pa[:, 0:128], [pa], [sc])
                pc = PSF.next()
                P.mm(pc[:, 0:4], cm(Ud), dt[:, 1, d * 4:d * 4 + 4], True, True, [cf, dt], [pc])
                ac = acr.next()
                P.copy("act", ac[:, 0, :], pc[:, 0:4], [pc], [ac])
                P.ts("dve", ac[:, 1, :], pc[:, 0:4], -1.0, None, ALU.mult, None, [pc], [ac])
                P.act(ac[:, 2, :], pc[:, 0:4], AF.Exp, [pc], [ac])
                vb = vbr.next()
                for h in range(4):
                    P.ts("dve", vb[:, h * 64:(h + 1) * 64], raw[:, h * 64:(h + 1) * 64],
                         dt[:, 0, d * 4 + h:d * 4 + h + 1], None, ALU.mult, None, [raw, dt], [vb])
                for h in range(4):
                    ab = abr.next()
                    P.ts("pool", ab[:], cm(CI_ONES), dt[:, 1, d * 4 + h:d * 4 + h + 1], None, ALU.mult, None,
                         [cf, dt], [ab])
                    pb = PSF.next()
                    P.mm(pb[:, 0:128], ab[:], cm(Ud), True, False, [ab, cf], [pb])
                    P.mm(pb[:, 0:128], cm(CI_ID), cm(CI_NEGF + d), False, True, [cf], [pb])
                    mk = mkr.next()
                    P.act(mk[:], pb[:, 0:128], AF.Exp, [pb, ac], [mk], bias=ac[:, 1, h:h + 1], scale=1.0)
                    al = alr.next()
                    P.copy("act", al[:, 0:1], pb[:, endcol:endcol + 1], [pb], [al])
                    P.act(al[:, 1:2], al[:, 0:1], AF.Exp, [al], [al])
                    P.act(ac[:, 3, h:h + 1], ac[:, 0, h:h + 1], AF.Exp, [ac, al], [ac], bias=al[:, 0:1], scale=-1.0)
                    ke = ker.next()
                    P.ts("dve", ke[:], raw[:, 256:384], ac[:, 3, h:h + 1], None, ALU.mult, None, [raw, ac], [ke])
                    am = W["am"].next()
                    P.tt("dve", am[:], sc[:], mk[:], ALU.mult, [sc, mk], [am])
                    sl = slice(h * 64, (h + 1) * 64)
                    py1 = PSF.next()
                    P.mm(py1[:, 0:64], am[:], vb[:, sl], True, True, [am, vb], [py1])
                    py2 = PSF.next()
                    P.mm(py2[:, 0:64], bt[:, 1, :], Sb[:, 0, sl], True, True, [bt, Sb], [py2])
                    y1 = W["y1"].next()
                    P.copy("act", y1[:, 0:64], py1[:, 0:64], [py1], [y1])
                    yap = ybuf[:, ti, sl]
                    if d == 0:
                        P.stt("dve", yap, py2[:, 0:64], ac[:, 2, h:h + 1], y1[:, 0:64], ALU.mult, ALU.add,
                              [py2, ac, y1], [ybuf])
                    else:
                        y2 = W["y2"].next()
                        P.stt("dve", y2[:, 0:64], py2[:, 0:64], ac[:, 2, h:h + 1], y1[:, 0:64], ALU.mult, ALU.add,
                              [py2, ac, y1], [y2])
                        P.tt("pool", yap, yap, y2[:, 0:64], ALU.add, [ybuf, y2], [ybuf])
                    ps = PSF.next()
                    P.mm(ps[:, 0:64], ke[:], vb[:, sl], True, True, [ke, vb], [ps])
                    P.stt("dve", S[:, 0, sl], S[:, 0, sl], al[:, 1:2], ps[:, 0:64], ALU.mult, ALU.add,
                          [S, al, ps], [S])
                    P.copy("act", Sb[:, 0, sl], S[:, 0, sl], [S], [Sb])
        xsr = Ring([C2.sb([128, 256], F32, "xs") for _ in range(2)])

        def pre(y, tok0):
            xs = xsr.next()
            P.dma(xs[:], ptm[tok0:tok0 + 128, 1536:1792], [env["d_ptm"]], [xs])
            P.tt("pool", xs[:], xs[:], rw[:, 0, :], ALU.mult, [xs, rw], [xs])
            P.tt("dve", y, y, xs[:], ALU.add, [ybuf, xs], [ybuf])
        for ti in range(NTA):
            finish_ssd(env, W, ybuf, ti, pre, rw)


def finish_ssd(env, W, ybuf, ti, pre, rw):
    P, nc, ptm, Y = env["P"], env["nc"], env["ptm"], env["Y"]
    tok0 = env["tiles"][ti][0]
    g = W["g"].next()
    P.dma(g[:], ptm[tok0:tok0 + 128, 1024 + 256:1024 + 512], [env["d_ptm"]], [g])
    y = ybuf[:, ti, :]
    pre(y, tok0)
    sg = W["sg"].next()
    P.act(sg[:], g[:], AF.Sigmoid, [g], [sg])
    P.tt("pool", g[:], g[:], sg[:], ALU.mult, [g, sg], [g])
    P.tt("dve", y, y, g[:], ALU.mult, [ybuf, g], [ybuf])
    rs = W["rs"]
    row_rsqrt_mean(P, W, ybuf, y, 256, rs)
    ob = W["ob"].next()
    P.stt("dve", ob[:], y, rs[:, 0:1], rw[:, 1, :], ALU.mult, ALU.mult, [ybuf, rs, rw], [ob])
    P.dma(Y[tok0:tok0 + 128, 256:512], ob[:], [ob], [env["d_Y"]], q="pool")


def mix_hyena(env, hy_w1, hy_w2, hy_w3, hy_bf, hy_w4, hy_skip, hy_negd, zx, zc, tx, tcx, kxx, kxc, d_kx):
    nc, P, cf, cbf, ptm, PSF, Y = env["nc"], env["P"], env["cf"], env["cbf"], env["ptm"], env["PSF"], env["Y"]
    NTL, LAT = env["NTL"], env["LAT"]
    with ExitStack() as es2:
        C2 = Ctx(nc, es2)
        w1 = C2.sb([33, 64], F32, "w1"); P.dma(w1[:], hy_w1, (), [w1])
        w2 = C2.sb([64, 64], F32, "w2"); P.dma(w2[:], hy_w2, (), [w2])
        w3 = C2.sb([64, 64], F32, "w3"); P.dma(w3[:], hy_w3, (), [w3])
        bf = C2.sb([64, 6], F32, "bf"); P.dma(bf[:], hy_bf, (), [bf])
        w4 = C2.sb([64, 4, 256], F32, "w4"); P.dma(w4[:], hy_w4, (), [w4])
        negd = C2.sb([128, 2], F32, "negd"); P.dma(negd[:], hy_negd, (), [negd])
        fb = C2.sb([64, 3], F32, "fb")
        P.tt("dve", fb[:], bf[:, 0:3], bf[:, 3:6], ALU.mult, [bf], [fb])
        fsc = C2.sb([64, 4, 3], F32, "fsc")
        P.ts("dve", fsc[:, 0, :], bf[:, 3:6], 0.5, None, ALU.mult, None, [bf], [fsc])
        P.ts("dve", fsc[:, 1, :], fb[:], 0.5, None, ALU.mult, None, [fb], [fsc])
        P.ts("dve", fsc[:, 2, :], bf[:, 3:6], 0.25, None, ALU.mult, None, [bf], [fsc])
        P.ts("dve", fsc[:, 3, :], fb[:], 0.25, None, ALU.mult, None, [fb], [fsc])
        ztr = Ring([C2.sb([33, 512], F32, "zt") for _ in range(2)])
        trr = Ring([C2.sb([128, 512], F32, "tr") for _ in range(2)])
        ur = Ring([C2.sb([64, 512], F32, "u") for _ in range(2)])
        hr = Ring([C2.sb([64, 512], F32, "h") for _ in range(3)])
        winr = Ring([C2.sb([128, 512], F32, "win") for _ in range(2)])
        kbr = Ring([C2.sb([128, 512], BF16, "kb") for _ in range(3)])
        ws = [w1, w2, w3]
        for (L, zsrc, tsrc, kx) in ((LAT, zx, tx, kxx), (CTX, zc, tcx, kxc)):
            NB = min(512, L)
            for dirn in range(2):
                for blk in range(L // NB):
                    n0 = blk * NB
                    zt = ztr.next()
                    P.dma(zt[:, 0:NB], zsrc[dirn, :, n0:n0 + NB], (), [zt])
                    tr_ = trr.next()
                    P.dma(tr_[:, 0:NB], tsrc[dirn:dirn + 1, n0:n0 + NB].to_broadcast([128, NB]), (), [tr_])
                    hb, hap = zt, zt[:, 0:NB]
                    for l in range(3):
                        ps = PSF.next()
                        P.mm(ps[0:64, 0:NB], ws[l][:], hap, True, True, [ws[l], hb], [ps])
                        u = ur.next()
                        P.act(u[:, 0:NB], ps[0:64, 0:NB], AF.Sin, [ps, fsc], [u],
                              bias=fsc[:, 3, l:l + 1], scale=fsc[:, 2, l:l + 1])
                        hh = hr.next()
                        P.act(hh[:, 0:NB], ps[0:64, 0:NB], AF.Sin, [ps, fsc], [hh],
                              bias=fsc[:, 1, l:l + 1], scale=fsc[:, 0, l:l + 1])
                        P.tt("dve", u[:, 0:NB], u[:, 0:NB], u[:, 0:NB], ALU.mult, [u], [u])
                        P.ts("dve", u[:, 0:NB], u[:, 0:NB], -4.0, 2.0, ALU.mult, ALU.add, [u], [u])
                        P.tt("dve", hh[:, 0:NB], hh[:, 0:NB], u[:, 0:NB], ALU.mult, [hh, u], [hh])
                        hb, hap = hh, hh[:, 0:NB]
                    for o in range(2):
                        for cc in range(2):
                            ps = PSF.next()
                            P.mm(ps[:, 0:NB], w4[:, o * 2 + dirn, cc * 128:(cc + 1) * 128], hap, True, True,
                                 [w4, hb], [ps])
                            win = winr.next()
                            P.act(win[:, 0:NB], tr_[:, 0:NB], AF.Exp, [tr_, negd], [win], scale=negd[:, cc:cc + 1])
                            kb = kbr.next()
                            P.tt("dve", kb[:, 0:NB], ps[:, 0:NB], win[:, 0:NB], ALU.mult, [ps, win], [kb])
                            r0 = o * 256 + cc * 128
                            if dirn == 0:
                                c0, w_ = (L - 1) + n0, NB
                            else:
                                c0, w_ = n0, (NB - 1 if n0 + NB == L else NB)
                            P.dma(kx[r0:r0 + 128, c0:c0 + w_], kb[:, 0:w_], [kb], [d_kx], q="pool")
        Sync.barrier(P)
    CG = 32
    with ExitStack() as es2:
        C2 = Ctx(nc, es2)
        sk = C2.sb([128, 2, 256], F32, "sk")
        P.dma(sk[:], hy_skip.to_broadcast([128, 2, 256]), (), [sk])
        NJM = max(NTL, 2)
        bufs = {n_: C2.sb([128, NJM, CG], F32, n_) for n_ in ("v", "x1", "x2", "gt", "cv", "ta")}
        zb = C2.sb([128, NJM * CG], BF16, "zb")
        zr = C2.sb([128, NJM, CG], BF16, "zr")
        ob = C2.sb([128, NJM, CG], BF16, "ob")
        tring = Ring([C2.sb([128, 128 * (2 * NJM - 1)], BF16, "T") for _ in range(2)])
        for (NJ, tokbase, kx, L) in ((NTL, CTX, kxx, LAT), (2, 0, kxc, CTX)):
            ncols = 128 * (2 * NJ - 1)
            order = [NJ - 1] + [x for x in range(2 * NJ - 1) if x != NJ - 1]
            for cg in range(256 // CG):
                c0 = cg * CG
                for nm, col in (("v", 0), ("x1", 256), ("x2", 512), ("gt", 1024)):
                    b_ = bufs[nm]
                    src = ptm[tokbase:tokbase + 128 * NJ, col + c0:col + c0 + CG].rearrange("(J j) c -> j J c", j=128)
                    P.dma(b_[:, 0:NJ, :], src, [env["d_ptm"]], [b_])
                seq = [("v", "x1", "ta"), ("ta", "x2", "v")]
                for o in range(2):
                    zn, xn, on = seq[o]
                    z, xo, outb = bufs[zn], bufs[xn], bufs[on]
                    cv = bufs["cv"]
                    nfl = NJ * CG
                    P.copy("act", zb[:, 0:nfl], z[:, 0:NJ, :].rearrange("p j c -> p (j c)"), [z], [zb])
                    for f0 in range(0, nfl, 512):
                        fw = min(512, nfl - f0)
                        ps = PSF.next()
                        P.mm(ps[:, 0:fw], cbf[:, 1, :], zb[:, f0:f0 + fw], True, True, [cbf, zb], [ps])
                        P.copy("act", zr[:, 0:NJ, :].rearrange("p j c -> p (j c)")[:, f0:f0 + fw], ps[:, 0:fw],
                               [ps], [zr])
                    for c in range(CG):
                        T = tring.next()
                        row = o * 256 + c0 + c
                        src = bass.AP(kx.tensor, row * 2 * L, [[1, 128], [1, ncols]])
                        P.dma(T[:, 0:ncols], src, [d_kx], [T])
                        pc = PSF.next()
                        for idx, Dp in enumerate(order):
                            Dl = Dp - (NJ - 1)
                            I0 = max(0, Dl)
                            I1 = min(NJ - 1, NJ - 1 + Dl)
                            n = I1 - I0 + 1
                            J0 = I0 - Dl
                            P.mm(pc[:, I0:I1 + 1], T[:, 128 * Dp:128 * Dp + 128], zr[:, J0:J0 + n, c],
                                 idx == 0, idx == len(order) - 1, [T, zr], [pc])
                        P.copy("act", cv[:, 0:NJ, c], pc[:, 0:NJ], [pc], [cv])
                    skb = sk[:, o, c0:c0 + CG].unsqueeze(1).to_broadcast([128, NJ, CG])
                    P.tt("dve", outb[:, 0:NJ, :], z[:, 0:NJ, :], skb, ALU.mult, [z, sk], [outb])
                    P.tt("dve", cv[:, 0:NJ, :], cv[:, 0:NJ, :], outb[:, 0:NJ, :], ALU.add, [cv, outb], [cv])
                    P.tt("dve", outb[:, 0:NJ, :], xo[:, 0:NJ, :], cv[:, 0:NJ, :], ALU.mult, [xo, cv], [outb])
                z2, gt, ta = bufs["v"], bufs["gt"], bufs["ta"]
                P.act(ta[:, 0:NJ, :], gt[:, 0:NJ, :], AF.Sigmoid, [gt], [ta])
                P.tt("dve", gt[:, 0:NJ, :], gt[:, 0:NJ, :], ta[:, 0:NJ, :], ALU.mult, [gt, ta], [gt])
                P.tt("dve", ob[:, 0:NJ, :], z2[:, 0:NJ, :], gt[:, 0:NJ, :], ALU.mult, [z2, gt], [ob])
                dst = Y[tokbase:tokbase + 128 * NJ, c0:c0 + CG].rearrange("(J j) c -> j J c", j=128)
                P.dma(dst, ob[:, 0:NJ, :], [ob], [env["d_Y"]], q="pool")


IN_SIZES = (3072, 1024, 2048, 32, 1024, 512, 512, 1024, 32, 1024, 1024, 1024, 1024, 1024)
OFF = np.concatenate([[0], np.cumsum(IN_SIZES)]).astype(np.int64)


def _fm(a):
    a = np.asarray(a)
    return np.ascontiguousarray(a.T.reshape(-1, 128, a.shape[0]).transpose(1, 0, 2))


def _pk(v):
    return np.ascontiguousarray(np.asarray(v).reshape(-1, 128).T)


def core_cols(q):
    r = np.arange
    g = []
    g.append(np.concatenate([OFF[0] + 256 * q + r(256), OFF[0] + 1024 + 256 * q + r(256)]))
    g.append(np.concatenate([OFF[0] + 2048 + 256 * q + r(256), -np.ones(256, np.int64)]))
    g.append(np.concatenate([OFF[1] + 256 * q + r(256), OFF[4] + 256 * q + r(256)]))
    g.append(np.concatenate([OFF[2] + 256 * q + r(256), OFF[2] + 1024 + 128 * q + r(128),
                             OFF[2] + 1536 + 128 * q + r(128)]))
    g.append(np.concatenate([OFF[5] + 128 * q + r(128), OFF[6] + 128 * q + r(128), OFF[7] + 256 * q + r(256)]))
    dtc = np.array([OFF[3] + d * 16 + 4 * q + rr for d in range(2) for rr in range(4)])
    g.append(np.concatenate([OFF[9] + 256 * q + r(256), dtc, OFF[8] + r(32), -np.ones(216, np.int64)]))
    g.append(np.concatenate([OFF[10] + 256 * q + r(256), OFF[11] + 256 * q + r(256)]))
    g.append(np.concatenate([OFF[12] + 256 * q + r(256), OFF[13] + 256 * q + r(256)]))
    return np.concatenate(g).astype(np.int64)


def hy_tables(L):
    t = (np.arange(L, dtype=np.float32) / np.float32(L)).astype(np.float64)
    bands = np.arange(1, 17, dtype=np.float64)[None, :]
    z = np.concatenate([t[:, None], np.cos(2 * np.pi * bands * t[:, None]), np.sin(2 * np.pi * bands * t[:, None])], 1)
    zT = z.T.astype(np.float32)
    return np.stack([zT, zT[:, ::-1]]).copy(), np.stack([t, t[::-1]]).astype(np.float32).copy()


def rope_tables(LAT):
    t = np.arange(LAT)
    row = (t // 64).astype(np.float64)[:, None]
    col = (t % 64).astype(np.float64)[:, None]
    inv = 10000.0 ** (-np.arange(0, 128, 2, dtype=np.float64) / 128)[None, :]
    ar, ac = row * inv, col * inv
    cos = np.concatenate([np.cos(ar), np.cos(ar), np.cos(ac), np.cos(ac)], 1)
    sin = np.concatenate([-np.sin(ar), np.sin(ar), -np.sin(ac), np.sin(ac)], 1)
    return np.ascontiguousarray(np.stack([cos, sin], 1)).astype(np.float32)


_PROG = {}


def _prog(key, fn):
    if key not in _PROG:
        _PROG[key] = fn()
    return _PROG[key]


def run_ada(inp, KC):
    D = 128 * KC
    c_all = np.zeros((4, D), np.float32)
    c_all[0:2] = inp["c"]
    c_all[2] = inp["c_ctx"]
    chunks = [(l, n) for l in range(DEPTH) for n in range(3 * KC)]
    NCH = -(-len(chunks) // 8)
    nc = _prog(("ada", KC, NCH), lambda: build_ada(KC, NCH))
    cT = np.ascontiguousarray(c_all.T.reshape(KC, 128, 4).transpose(1, 0, 2))
    maps, owner = [], []
    for core in range(8):
        ws, bs, own = [], [], []
        for i in range(NCH):
            l, n = chunks[(core * NCH + i) % len(chunks)]
            ws.append(inp["w_ada"][l][:, n * 128:(n + 1) * 128].reshape(KC, 128, 128).transpose(1, 0, 2))
            bs.append(inp["b_ada"][l][n * 128:(n + 1) * 128])
            own.append((l, n))
        maps.append({"cT": cT, "w": np.ascontiguousarray(np.stack(ws)), "b": np.ascontiguousarray(np.stack(bs, 1))})
        owner.append(own)
    res = run_bass_kernel_spmd(nc, maps, core_ids=list(range(8)))
    mods = np.zeros((DEPTH, 3 * D, 4), np.float32)
    for core in range(8):
        o = res.results[core]["o"]
        for i, (l, n) in enumerate(owner[core]):
            mods[l, n * 128:(n + 1) * 128, :] = o[:, i, :]
    return mods


def run_mix(inp, l, xcat, mods, KC, LAT, stages=None):
    D = 128 * KC
    kw = {} if stages is None else {"stages": stages}
    nc = _prog(("mix", KC, LAT, stages), lambda: build_mix(KC, LAT, **kw))
    constf, constb = host_constf(), host_constb()
    zx, tx = hy_tables(LAT)
    zc, tcx = hy_tables(CTX)
    rcs = rope_tables(LAT)
    ii = np.arange(128, dtype=np.float32)
    ret_pos = np.stack([ii + 1, 127 - ii, 128 - ii, ii], 1).astype(np.float32)
    deltas = np.abs(np.linspace(math.log(1e-2) / 1.5, math.log(1e-2) / 0.3, 1024))
    maps = []
    for core in range(8):
        b, q = core // 4, core % 4
        cols = core_cols(q)
        w = inp["w_in"][l]
        wcore = np.where(cols[None, :] >= 0, w[:, np.maximum(cols, 0)], 0.0).astype(np.float32)
        wcm = np.ascontiguousarray(wcore.reshape(KC, 128, 8, 512).transpose(2, 1, 0, 3))
        cw = np.zeros((3, 3, 512), np.float32)
        cb = np.zeros((1, 3, 512), np.float32)
        hcw, hcb = inp["hy_conv_w"][l], inp["hy_conv_b"][l]
        scw, scb = inp["ssd_conv_w"][l], inp["ssd_conv_b"][l]
        r = np.arange
        i0 = np.concatenate([256 * q + r(256), 1024 + 256 * q + r(256)])
        i1 = 2048 + 256 * q + r(256)
        i3 = np.concatenate([256 * q + r(256), 1024 + 128 * q + r(128), 1536 + 128 * q + r(128)])
        cw[:, 0, :] = hcw[:, i0]; cb[0, 0, :] = hcb[i0]
        cw[:, 1, 0:256] = hcw[:, i1]; cb[0, 1, 0:256] = hcb[i1]
        cw[:, 2, :] = scw[:, i3]; cb[0, 2, :] = scb[i3]
        mm_ = mods[l]
        md = np.stack([_pk(mm_[D:2 * D, b]), _pk(mm_[0:D, b]), _pk(mm_[D:2 * D, 2]), _pk(mm_[0:D, 2])], 1)
        hy_bf = np.stack([inp["hy_b1"][l], inp["hy_b2"][l], inp["hy_b3"][l],
                          inp["hy_freq"][l][0], inp["hy_freq"][l][1], inp["hy_freq"][l][2]], 1)
        w4 = inp["hy_w4"][l].reshape(64, 2, 2, 1024)[:, :, :, q * 256:(q + 1) * 256].reshape(64, 4, 256)
        hd = [4 * q + rr for rr in range(4)]
        ssd_pr = np.zeros((1, 4, 8), np.float32)
        ssd_pr[0, 0] = inp["ssd_a_log"][l][:, hd].reshape(8)
        ssd_pr[0, 1] = inp["ssd_dt_bias"][l][:, hd].reshape(8)
        ssd_row = np.stack([np.repeat(inp["ssd_d"][l][hd], 64), inp["ssd_norm_w"][l][q * 256:(q + 1) * 256]])[None]
        maps.append({
            "xT": _fm(xcat[b]), "mods": np.ascontiguousarray(md.astype(np.float32)), "wc": wcm, "cw": cw, "cb": cb,
            "constf": constf, "constb": constb,
            "hy_w1": np.ascontiguousarray(inp["hy_w1"][l]), "hy_w2": np.ascontiguousarray(inp["hy_w2"][l]),
            "hy_w3": np.ascontiguousarray(inp["hy_w3"][l]), "hy_bf": np.ascontiguousarray(hy_bf.astype(np.float32)),
            "hy_w4": np.ascontiguousarray(w4), "hy_skip": np.ascontiguousarray(inp["hy_skip"][l][:, q * 256:(q + 1) * 256][None]),
            "hy_negd": np.ascontiguousarray((-deltas[q * 256:(q + 1) * 256]).reshape(2, 128).T.astype(np.float32)),
            "zx": zx, "zc": zc, "tx": tx, "tcx": tcx,
            "ssd_pr": ssd_pr, "ssd_row": np.ascontiguousarray(ssd_row.astype(np.float32)),
            "gla_w2": np.ascontiguousarray(inp["gla_w2"][l][:, :, q * 128:(q + 1) * 128].transpose(1, 0, 2)),
            "gla_b2": np.ascontiguousarray(inp["gla_b2"][l][:, q * 128:(q + 1) * 128][None]),
            "gla_nw": np.ascontiguousarray(inp["gla_norm_w"][l][None]),
            "ret_dec": np.ascontiguousarray(inp["ret_decay"][l][:, q][None]), "ret_pos": ret_pos, "ret_cs": rcs,
        })
    res = run_bass_kernel_spmd(nc, maps, core_ids=list(range(8)))
    NTOK = CTX + LAT
    yfull = np.zeros((2, NTOK, 4, 4, 256), NPBF)
    for core in range(8):
        b, q = core // 4, core % 4
        Yc = res.results[core]["Y"]
        yfull[b, :, :, q, :] = Yc.reshape(NTOK, 4, 256)
    return yfull.reshape(2, NTOK, 4096)


def run_post(inp, l, xcat, yfull, mods, KC, LAT, with_ctx):
    D = 128 * KC
    NCT = CTX // 4 if with_ctx else 0
    NLT = LAT // 4
    nc = _prog(("post", KC, NCT, NLT), lambda: build_post(KC, NCT, NLT))
    wg = np.ascontiguousarray(inp["w_gate"][l].reshape(4, KC, 128, KC, 128).transpose(0, 3, 2, 1, 4))
    wb = np.ascontiguousarray(inp["w_br"][l].reshape(4, 8, 128, KC, 128).transpose(0, 3, 2, 1, 4))
    wo = np.ascontiguousarray(inp["w_out"][l].reshape(KC, 128, KC, 128).transpose(2, 1, 0, 3))
    ones = np.ones((128, 128), np.float32)
    mm_ = mods[l]
    maps, toks = [], []
    for core in range(8):
        b, q = core // 4, core % 4
        idx = np.concatenate([NCT * q + np.arange(NCT), CTX + NLT * q + np.arange(NLT)]).astype(np.int64)
        toks.append((b, idx))
        md = np.stack([_pk(mm_[D:2 * D, b]), _pk(mm_[0:D, b]), _pk(mm_[2 * D:3 * D, b]),
                       _pk(mm_[D:2 * D, 2]), _pk(mm_[0:D, 2]), _pk(mm_[2 * D:3 * D, 2]),
                       _pk(inp["ln_g"][l]), _pk(inp["ln_b"][l])], 1).astype(np.float32)
        maps.append({"xT": _fm(xcat[b][idx]), "yT": _fm(yfull[b][idx]), "mods": np.ascontiguousarray(md),
                     "wg": wg, "wb": wb, "wo": wo, "onesf": ones})
    res = run_bass_kernel_spmd(nc, maps, core_ids=list(range(8)))
    xnew = [x.copy() for x in xcat]
    for core in range(8):
        b, idx = toks[core]
        o = res.results[core]["out"]
        xnew[b][idx] = o.transpose(1, 0, 2).reshape(D, len(idx)).T
    return xnew


def run_module(inp, KC, LAT):
    inp = {k: np.asarray(v) for k, v in inp.items()}
    mods = run_ada(inp, KC)
    xcat = [np.concatenate([inp["ctx"][b], inp["x"][b]], 0).astype(np.float32) for b in range(2)]
    for l in range(DEPTH):
        yfull = run_mix(inp, l, xcat, mods, KC, LAT)
        xcat = run_post(inp, l, xcat, yfull, mods, KC, LAT, with_ctx=(l < DEPTH - 1))
    return np.stack([xcat[b][CTX:] for b in range(2)]).astype(np.float32)


def kernel(**inputs):
    return run_module(inputs, 32, 8192)
```

```python
import math
from contextlib import ExitStack
import numpy as np
import ml_dtypes
import concourse.bass as bass
import concourse.mybir as mybir
from concourse.bass_utils import run_bass_kernel_spmd

F32 = mybir.dt.float32
BF16 = mybir.dt.bfloat16
AF = mybir.ActivationFunctionType
ALU = mybir.AluOpType
AX = mybir.AxisListType
NPBF = ml_dtypes.bfloat16

EPS = 1e-6
DEPTH = 2
ALPHA = (2 * DEPTH) ** 0.25
CTX = 256
PI = math.pi


class Buf:
    def __init__(self, t):
        self.t = t
        self.writers = {}
        self.readers = {}

    def __getitem__(self, idx):
        return self.t[idx]


class Ring:
    def __init__(self, bufs):
        self.bufs = bufs
        self.i = 0

    def next(self):
        b = self.bufs[self.i % len(self.bufs)]
        self.i += 1
        return b


class Prog:
    def __init__(self, nc, n_dma_sems=8):
        self.nc = nc
        self.eng = {"pe": nc.tensor, "act": nc.scalar, "dve": nc.vector, "pool": nc.gpsimd, "sp": nc.sync}
        self.sem, self.cnt, self._ctx = {}, {}, []
        for k in ["pe", "act", "dve", "pool"] + ["dma%d" % i for i in range(n_dma_sems)]:
            cm = nc.semaphore("s_" + k)
            self.sem[k] = cm.__enter__()
            self._ctx.append(cm)
            self.cnt[k] = 0
        self.dma_keys = ["dma%d" % i for i in range(n_dma_sems)]
        self.dma_rr = 0
        self.seen = {e: {} for e in self.eng}
        self.n_inst = 0

    def close(self):
        for cm in reversed(self._ctx):
            cm.__exit__(None, None, None)

    def _wait(self, e, deps):
        for k, v in deps.items():
            if v <= 0 or self.seen[e].get(k, 0) >= v:
                continue
            self.eng[e].wait_ge(self.sem[k], v)
            self.seen[e][k] = v

    @staticmethod
    def _deps(reads, writes):
        deps = {}
        for r in reads:
            for k, v in r.writers.items():
                if deps.get(k, 0) < v:
                    deps[k] = v
        for w in writes:
            for d in (w.writers, w.readers):
                for k, v in d.items():
                    if deps.get(k, 0) < v:
                        deps[k] = v
        return deps

    @staticmethod
    def _commit(key, val, reads, writes):
        for r in reads:
            r.readers[key] = val
        for w in writes:
            w.writers[key] = val
            w.readers = {}

    def op(self, e, ins_fn, reads=(), writes=()):
        deps = self._deps(reads, writes)
        if e == "pe":
            deps.pop("pe", None)
        self._wait(e, deps)
        ins = ins_fn()
        self.cnt[e] += 1
        ins.then_inc(self.sem[e], 1)
        self._commit(e, self.cnt[e], reads, writes)
        self.n_inst += 1

    def dma(self, out, in_, reads=(), writes=(), q="sp", **kw):
        k = self.dma_keys[self.dma_rr % len(self.dma_keys)]
        self.dma_rr += 1
        deps = self._deps(reads, writes)
        if self.cnt[k] > 0:
            deps[k] = max(deps.get(k, 0), self.cnt[k])
        self._wait(q, deps)
        ins = self.eng[q].dma_start(out=out, in_=in_, **kw)
        self.cnt[k] += 16
        ins.then_inc(self.sem[k], 16)
        self._commit(k, self.cnt[k], reads, writes)
        self.n_inst += 1

    def finish(self, bufs):
        deps = {}
        for b in bufs:
            for k, v in b.writers.items():
                deps[k] = max(deps.get(k, 0), v)
        self._wait("sp", deps)

    def mm(self, out, lhsT, rhs, start, stop, reads, writes):
        nc = self.nc
        self.op("pe", lambda: nc.tensor.matmul(out, lhsT=lhsT, rhs=rhs, start=start, stop=stop), reads, writes)

    def tr(self, out, in_, ident, reads, writes):
        nc = self.nc
        self.op("pe", lambda: nc.tensor.transpose(out, in_, ident), reads, writes)

    def act(self, out, in_, func, reads, writes, bias=None, scale=None):
        nc = self.nc
        kw = {}
        if bias is not None:
            kw["bias"] = bias
        if scale is not None:
            kw["scale"] = scale
        self.op("act", lambda: nc.scalar.activation(out=out, in_=in_, func=func, **kw), reads, writes)

    def tt(self, e, out, in0, in1, op, reads, writes):
        en = self.eng[e]
        self.op(e, lambda: en.tensor_tensor(out=out, in0=in0, in1=in1, op=op), reads, writes)

    def ts(self, e, out, in0, s1, s2, op0, op1, reads, writes):
        en = self.eng[e]
        if s2 is None:
            self.op(e, lambda: en.tensor_scalar(out=out, in0=in0, scalar1=s1, scalar2=None, op0=op0), reads, writes)
        else:
            self.op(e, lambda: en.tensor_scalar(out=out, in0=in0, scalar1=s1, scalar2=s2, op0=op0, op1=op1),
                    reads, writes)

    def stt(self, e, out, in0, scalar, in1, op0, op1, reads, writes):
        en = self.eng[e]
        self.op(e, lambda: en.scalar_tensor_tensor(out=out, in0=in0, scalar=scalar, in1=in1, op0=op0, op1=op1),
                reads, writes)

    def copy(self, e, out, in_, reads, writes):
        if e == "act":
            self.act(out, in_, AF.Copy, reads, writes)
        else:
            en = self.eng[e]
            self.op(e, lambda: en.tensor_copy(out=out, in_=in_), reads, writes)

    def memset(self, e, buf, ap, val):
        en = self.eng[e]
        self.op(e, lambda: en.memset(ap, val), (), [buf])


class Ctx:
    _N = [0]

    def __init__(self, nc, es):
        self.nc, self.es = nc, es

    def sb(self, shape, dt, name=None):
        Ctx._N[0] += 1
        return Buf(self.es.enter_context(self.nc.sbuf_tensor("%s_%d" % (name or "sb", Ctx._N[0]), list(shape), dt)))

    def ps(self, shape, dt, name=None):
        Ctx._N[0] += 1
        return Buf(self.es.enter_context(self.nc.psum_tensor("%s_%d" % (name or "ps", Ctx._N[0]), list(shape), dt)))

    def din(self, name, shape, dt=F32):
        return self.nc.dram_tensor(name, list(shape), dt, kind="ExternalInput").ap()

    def dout(self, name, shape, dt=F32):
        return self.nc.dram_tensor(name, list(shape), dt, kind="ExternalOutput").ap()

    def dscr(self, name, shape, dt):
        return self.nc.dram_tensor(name, list(shape), dt).ap()


def bc_mid(ap2d, n):
    return ap2d.unsqueeze(1).to_broadcast([ap2d.shape[0], n, ap2d.shape[1]])


def emit_ln_fm(P, C, K, src, T, KC, sc_ap, sh_ap, out_buf, out_ap, sq, ps_ring, st):
    nc = P.nc
    D = 128 * KC
    ones = K["onesf"]
    P.act(sq[:, :, 0:T], src[:, :, 0:T], AF.Square, [src], [sq])
    p1 = ps_ring.next()
    p2 = ps_ring.next()
    for kc in range(KC):
        P.mm(p1[:, 0:T], ones[:, :], src[:, kc, 0:T], kc == 0, kc == KC - 1, [ones, src], [p1])
    for kc in range(KC):
        P.mm(p2[:, 0:T], ones[:, :], sq[:, kc, 0:T], kc == 0, kc == KC - 1, [ones, sq], [p2])
    mean, var, rstd, nmr = st
    P.ts("dve", mean[:, 0:T], p1[:, 0:T], 1.0 / D, None, ALU.mult, None, [p1], [mean])
    P.ts("dve", var[:, 0:T], p2[:, 0:T], 1.0 / D, None, ALU.mult, None, [p2], [var])
    P.tt("dve", rstd[:, 0:T], mean[:, 0:T], mean[:, 0:T], ALU.mult, [mean], [rstd])
    P.tt("dve", var[:, 0:T], var[:, 0:T], rstd[:, 0:T], ALU.subtract, [var, rstd], [var])
    emit_rsqrt(P, rstd, rstd[:, 0:T], var, var[:, 0:T], EPS)
    P.stt("dve", nmr[:, 0:T], mean[:, 0:T], -1.0, rstd[:, 0:T], ALU.mult, ALU.mult, [mean, rstd], [nmr])
    P.tt("dve", src[:, :, 0:T], src[:, :, 0:T], bc_mid(rstd[:, 0:T], KC), ALU.mult, [src, rstd], [src])
    P.tt("dve", src[:, :, 0:T], src[:, :, 0:T], bc_mid(nmr[:, 0:T], KC), ALU.add, [src, nmr], [src])
    for kc in range(KC):
        P.act(out_ap(kc), src[:, kc, 0:T], AF.Identity, [src], [out_buf],
              bias=sh_ap[:, kc:kc + 1], scale=sc_ap[:, kc:kc + 1])


def emit_rsqrt(P, ob, o_ap, ib, i_ap, eps, scale=1.0):
    nc = P.nc
    P.ts("dve", i_ap, i_ap, scale, eps, ALU.mult, ALU.add, [ib], [ib])
    P.act(i_ap, i_ap, AF.Sqrt, [ib], [ib])
    P.op("dve", lambda: nc.vector.reciprocal(out=o_ap, in_=i_ap), [ib], [ob])


def load_consts(P, C, cin, names_f, names_b):
    K = {}
    for nm in names_f:
        ap = cin[nm]
        b = C.sb(ap.shape, F32, nm)
        P.dma(b[:], ap, (), [b])
        K[nm] = b
    for nm in names_b:
        ap = cin[nm]
        b = C.sb(ap.shape, BF16, nm)
        P.dma(b[:], ap, (), [b])
        K[nm] = b
    return K


def build_ada(KC, NCH):
    nc = bass.Bass("TRN2", target_bir_lowering=False)
    with ExitStack() as es:
        C = Ctx(nc, es)
        P = Prog(nc)
        cT = C.din("cT", [128, KC, 4])
        w = C.din("w", [NCH, 128, KC, 128])
        b = C.din("b", [128, NCH])
        o = C.dout("o", [128, NCH, 4])
        ct = C.sb([128, KC, 4], F32)
        sg = C.sb([128, KC, 4], F32)
        st = C.sb([128, KC, 4], F32)
        bt = C.sb([128, NCH], F32)
        ot = C.sb([128, NCH, 4], F32)
        wr = Ring([C.sb([128, KC, 128], F32) for _ in range(3)])
        pr = Ring([C.ps([128, 512], F32) for _ in range(2)])
        odr = Buf(None)
        P.dma(ct[:], cT, (), [ct])
        P.dma(bt[:], b, (), [bt])
        P.act(sg[:], ct[:], AF.Sigmoid, [ct], [sg])
        P.tt("dve", st[:], ct[:], sg[:], ALU.mult, [ct, sg], [st])
        for n in range(NCH):
            wt = wr.next()
            P.dma(wt[:], w[n], (), [wt])
            pp = pr.next()
            for kc in range(KC):
                P.mm(pp[:, 0:4], wt[:, kc, :], st[:, kc, :], kc == 0, kc == KC - 1, [wt, st], [pp])
            P.act(ot[:, n, :], pp[:, 0:4], AF.Identity, [pp, bt], [ot], bias=bt[:, n:n + 1], scale=1.0)
        P.dma(o, ot[:], [ot], [odr], q="pool")
        P.finish([odr])
        P.close()
    return nc


def build_post(KC, NCT, NLT, TP=256, fz=None):
    D = 128 * KC
    NTK = NCT + NLT
    nc = fz["nc"] if fz else bass.Bass("TRN2", target_bir_lowering=False)
    with ExitStack() as es:
        C = Ctx(nc, es)
        P = fz["P"] if fz else Prog(nc)
        sfx = fz["sfx"] if fz else ""
        wg = C.din("wg" + sfx, [4, KC, 128, KC, 128])
        wb = C.din("wb" + sfx, [4, KC, 128, 8, 128])
        wo = C.din("wo" + sfx, [KC, 128, KC, 128])
        md = C.sb([128, 8, KC], F32)
        odr = Buf(None)
        if fz:
            xT, out, Yfull = fz["xT"], fz["out"], fz["Yfull"]
            lnp = C.din("lnp" + sfx, [128, 2, KC])
            P.dma(md[:, 0:6, :], fz["mods_post"], [fz["dbuf"]["mods"]], [md])
            P.dma(md[:, 6:8, :], lnp, (), [md])
            cf, cbf = fz["cf"], fz["cbf"]

            class _Ones:
                writers = cf.writers
                readers = cf.readers

                def __getitem__(self, idx):
                    return cf[:, CI_ONES, :]
            K = {"onesf": _Ones()}
            psr, PSB = fz["PSF"], fz["PSB"]
            odr = fz["dbuf"]["xout"]
            yraw = Ring([C.sb([128, 4096], BF16, "yraw") for _ in range(2)])
        else:
            xT = C.din("xT", [128, KC, NTK])
            yT = C.din("yT", [128, 4 * 8, NTK], BF16)
            mods = C.din("mods", [128, 8, KC])
            onesd = C.din("onesf", [128, 128])
            out = C.dout("out", [128, KC, NTK])
            K = load_consts(P, C, {"onesf": onesd}, ["onesf"], [])
            P.dma(md[:], mods, (), [md])
            psr = Ring([C.ps([128, 512], F32) for _ in range(6)])
            wg_f, wb_f, wo_f = wg, wb, wo
            wg = C.dscr("wg_b", [4, KC, 128, KC, 128], BF16)
            wb = C.dscr("wb_b", [4, KC, 128, 8, 128], BF16)
            wo = C.dscr("wo_b", [KC, 128, KC, 128], BF16)
            d_w = Buf(None)
            for n in range(KC):
                for i in range(4):
                    P.dma(wg[i, n], wg_f[i, n], (), [d_w], q="pool")
                    P.dma(wb[i, n], wb_f[i, n], (), [d_w], q="pool")
                P.dma(wo[n], wo_f[n], (), [d_w], q="pool")
        wdep = () if fz else [d_w]
        wq = "pool" if fz else "sp"
        sc1 = C.sb([128, 2, KC], F32)
        P.ts("dve", sc1[:, 0, :], md[:, 0, :], 1.0, None, ALU.add, None, [md], [sc1])
        P.ts("dve", sc1[:, 1, :], md[:, 3, :], 1.0, None, ALU.add, None, [md], [sc1])

        xf = C.sb([128, KC, TP], F32, "xf")
        hT = C.sb([128, KC, TP], BF16, "hT")
        yt = C.sb([128, 32, TP], BF16, "yt")
        mT = C.sb([128, KC, TP], BF16, "mT")
        sT = C.sb([128, KC, TP], F32, "sT")
        st = [C.sb([128, TP], F32, "st%d" % i) for i in range(4)]
        wgr = Ring([C.sb([128, KC, 128], BF16, "wg") for _ in range(3)])
        wbr = Ring([C.sb([128, 8, 128], BF16, "wb") for _ in range(3)])
        sig = Ring([C.sb([128, TP], F32, "sig") for _ in range(2)])
        acc = C.sb([128, TP], F32, "acc")
        tmp = Ring([C.sb([128, TP], F32, "tmp") for _ in range(2)])
        passes = []
        if fz:
            passes = fz["passes"]
        else:
            if NCT:
                passes.append((0, NCT, 1, 0))
            t = NCT
            while t < NTK:
                n = min(TP, NTK - t)
                passes.append((t, n, 0, t))
                t += n
        xin_dep = [fz["dbuf"]["xin"]] if fz else ()
        for (t0, T, isctx, to0) in passes:
            P.dma(xf[:, :, 0:T], xT[:, :, t0:t0 + T], xin_dep, [xf])
            if fz:
                for j in range(T // 128):
                    yr = yraw.next()
                    P.dma(yr[:], Yfull[t0 + j * 128:t0 + (j + 1) * 128, :], [fz["dbuf"]["Y"]], [yr])
                    for f in range(32):
                        pt = PSB.next()
                        P.tr(pt[:, 0:128], yr[:, f * 128:(f + 1) * 128], cbf[:, 0, :], [yr, cbf], [pt])
                        P.copy("act" if f % 2 else "dve", yt[:, f, j * 128:(j + 1) * 128], pt[:, 0:128], [pt], [yt])
            else:
                P.dma(yt[:, :, 0:T], yT[:, :, t0:t0 + T], (), [yt])
            emit_ln_fm(P, C, K, xf, T, KC, sc1[:, isctx, :], md[:, 1 + 3 * isctx, :], hT,
                       lambda kc: hT[:, kc, 0:T], sT, psr, st)
            gate_ap = md[:, 2 + 3 * isctx, :]
            for n in range(KC):
                for i in range(4):
                    wgt = wgr.next()
                    P.dma(wgt[:], wg[i, n], wdep, [wgt], q=wq)
                    wbt = wbr.next()
                    P.dma(wbt[:], wb[i, n], wdep, [wbt], q=wq)
                    pg = psr.next()
                    for kc in range(KC):
                        P.mm(pg[:, 0:T], wgt[:, kc, :], hT[:, kc, 0:T], kc == 0, kc == KC - 1, [wgt, hT], [pg])
                    pb = psr.next()
                    for kc in range(8):
                        P.mm(pb[:, 0:T], wbt[:, kc, :], yt[:, i * 8 + kc, 0:T], kc == 0, kc == 7, [wbt, yt], [pb])
                    sg = sig.next()
                    P.act(sg[:, 0:T], pg[:, 0:T], AF.Sigmoid, [pg], [sg])
                    if i == 0:
                        P.tt("dve", acc[:, 0:T], sg[:, 0:T], pb[:, 0:T], ALU.mult, [sg, pb], [acc])
                    else:
                        tm = tmp.next()
                        P.tt("dve", tm[:, 0:T], sg[:, 0:T], pb[:, 0:T], ALU.mult, [sg, pb], [tm])
                        if i < 3:
                            P.tt("dve", acc[:, 0:T], acc[:, 0:T], tm[:, 0:T], ALU.add, [acc, tm], [acc])
                        else:
                            P.tt("dve", mT[:, n, 0:T], acc[:, 0:T], tm[:, 0:T], ALU.add, [acc, tm], [mT])
            P.dma(xf[:, :, 0:T], xT[:, :, t0:t0 + T], xin_dep, [xf])
            for n in range(KC):
                wot = wgr.next()
                P.dma(wot[:], wo[n], wdep, [wot], q=wq)
                po = psr.next()
                for kc in range(KC):
                    P.mm(po[:, 0:T], wot[:, kc, :], mT[:, kc, 0:T], kc == 0, kc == KC - 1, [wot, mT], [po])
                tm = tmp.next()
                P.ts("dve", tm[:, 0:T], po[:, 0:T], gate_ap[:, n:n + 1], None, ALU.mult, None, [po, md], [tm])
                P.stt("dve", sT[:, n, 0:T], xf[:, n, 0:T], ALPHA, tm[:, 0:T], ALU.mult, ALU.add, [xf, tm], [sT])
            emit_ln_fm(P, C, K, sT, T, KC, md[:, 6, :], md[:, 7, :], sT,
                       lambda kc: sT[:, kc, 0:T], xf, psr, st)
            P.dma(out[:, :, to0:to0 + T], sT[:, :, 0:T], [sT], [odr], q="pool")
        if not fz:
            P.finish([odr])
            P.close()
        else:
            Sync.barrier(P)
    return nc


CONV_GROUPS = (0, 1, 3)
(CI_ID, CI_ONES, CI_U, CI_UT, CI_NEGF, CI_NEGB, CI_G1F, CI_G2F, CI_G3F, CI_G1B, CI_G2B, CI_G3B,
 CI_LAGF, CI_LAGB, CI_N) = range(15)


def host_constf():
    j = np.arange(128)[:, None].astype(np.float64)
    i = np.arange(128)[None, :].astype(np.float64)
    U = (j <= i).astype(np.float64)
    UT = (j >= i).astype(np.float64)
    c = np.zeros((CI_N, 128, 128))
    c[CI_ID] = np.eye(128)
    c[CI_ONES] = 1.0
    c[CI_U] = U
    c[CI_UT] = UT
    c[CI_NEGF] = (U - 1.0) * 30000.0
    c[CI_NEGB] = (UT - 1.0) * 30000.0
    s = -1.0 / 16.0
    c[CI_G1F] = s * (U - U[:, 64:65])
    c[CI_G2F] = s * U
    c[CI_G3F] = s * (1.0 - U)
    c[CI_G1B] = s * (UT - UT[:, 63:64])
    c[CI_G2B] = s * UT
    c[CI_G3B] = s * (1.0 - UT)
    c[CI_LAGF] = np.maximum(i - j, 0.0)
    c[CI_LAGB] = np.maximum(j - i, 0.0)
    return np.ascontiguousarray(c.transpose(1, 0, 2)).astype(np.float32)


def host_constb():
    c = np.zeros((128, 2, 128), np.float32)
    c[:, 0, :] = np.eye(128)
    c[:, 1, :] = np.eye(128)[::-1]
    return c.astype(NPBF)


class Sync:
    @staticmethod
    def barrier(P):
        deps = {k: v for k, v in P.cnt.items() if v > 0}
        for e in P.eng:
            P._wait(e, dict(deps))


ALL_STAGES = ("ln", "wprep", "proj", "hy", "ssd", "gla", "ret")
SHARED_INS = ("constf", "constb", "zx", "zc", "tx", "tcx", "ret_pos", "ret_cs")


def build_mix(KC, LAT, stages=ALL_STAGES, fz=None):
    D = 128 * KC
    NTL = LAT // 128
    NTOK = CTX + LAT
    NTP = NTOK + 4
    NTA = 2 + NTL
    nc = fz["nc"] if fz else bass.Bass("TRN2", target_bir_lowering=False)
    with ExitStack() as es:
        C = Ctx(nc, es)
        P = fz["P"] if fz else Prog(nc)
        ctx0 = C

        class _D:
            @staticmethod
            def din(name, shape, dt=F32):
                if not fz:
                    return Ctx.din(ctx0, name, shape, dt)
                key = name if name in SHARED_INS else name + fz["sfx"]
                if key not in fz["dins"]:
                    fz["dins"][key] = Ctx.din(ctx0, key, shape, dt)
                return fz["dins"][key]
        xT = fz["xT"] if fz else C.din("xT", [128, KC, NTOK])
        mods = fz["mods"] if fz else C.din("mods", [128, 4, KC])
        C_ = C
        C = _D
        wc = C.din("wc", [8, 128, KC, 512])
        cw = C.din("cw", [3, 3, 512])
        cb = C.din("cb", [1, 3, 512])
        constf_d = C.din("constf", [128, CI_N, 128])
        constb_d = C.din("constb", [128, 2, 128], BF16)
        hy_w1 = C.din("hy_w1", [33, 64]); hy_w2 = C.din("hy_w2", [64, 64]); hy_w3 = C.din("hy_w3", [64, 64])
        hy_bf = C.din("hy_bf", [64, 6])
        hy_w4 = C.din("hy_w4", [64, 4, 256])
        hy_skip = C.din("hy_skip", [1, 2, 256])
        hy_negd = C.din("hy_negd", [128, 2])
        zx = C.din("zx", [2, 33, LAT]); zc = C.din("zc", [2, 33, CTX])
        tx = C.din("tx", [2, LAT]); tcx = C.din("tcx", [2, CTX])
        ssd_pr = C.din("ssd_pr", [1, 4, 8])
        ssd_row = C.din("ssd_row", [1, 2, 256])
        gla_w2 = C.din("gla_w2", [16, 2, 128]); gla_b2 = C.din("gla_b2", [1, 2, 128])
        gla_nw = C.din("gla_nw", [1, 256])
        ret_dec = C.din("ret_dec", [1, 2])
        ret_pos = C.din("ret_pos", [128, 4])
        ret_cs = C.din("ret_cs", [LAT, 2, 256])
        C = C_
        if fz:
            Y = fz["Y"]
            hTd, wbd, ptm, kxx, kxc = [fz["scr"][k_] for k_ in ("hTd", "wbd", "ptm", "kxx", "kxc")]
            d_hT, d_wb, d_ptm, d_kx, d_Y = [fz["dbuf"][k_] for k_ in ("hT", "wb", "ptm", "kx", "Y")]
            if "cf" not in fz:
                fz["cf"] = Buf(fz["es"].enter_context(nc.sbuf_tensor("cf_g", [128, CI_N, 128], F32)))
                P.dma(fz["cf"][:], constf_d, (), [fz["cf"]])
                fz["cbf"] = Buf(fz["es"].enter_context(nc.sbuf_tensor("cbf_g", [128, 2, 128], BF16)))
                P.dma(fz["cbf"][:], constb_d, (), [fz["cbf"]])
            cf, cbf = fz["cf"], fz["cbf"]
        else:
            Y = C.dout("Y", [NTOK, 1024], BF16)
            hTd, wbd, ptm, kxx, kxc = mix_scratch(C, KC, LAT)
            d_hT, d_wb, d_ptm, d_kx, d_Y = Buf(None), Buf(None), Buf(None), Buf(None), Buf(None)
            cf = C.sb([128, CI_N, 128], F32, "cf")
            P.dma(cf[:], constf_d, (), [cf])
            cbf = C.sb([128, 2, 128], BF16, "cbf")
            P.dma(cbf[:], constb_d, (), [cbf])
        K = {"onesf": cf}

        class _Ones:
            writers = cf.writers
            readers = cf.readers

            def __getitem__(self, idx):
                return cf[:, CI_ONES, :]
        K["onesf"] = _Ones()
        if fz:
            PSF, PSB = fz["PSF"], fz["PSB"]
        else:
            PSF = Ring([C.ps([128, 512], F32, "psf") for _ in range(6)])
            PSB = Ring([C.ps([128, 512], BF16, "psb") for _ in range(2)])

        def cm(i):
            return cf[:, i, :]

        blk_of = {}
        nb = 0
        for g in range(8):
            for tap in (range(3) if g in CONV_GROUPS else range(1)):
                blk_of[(g, tap)] = nb
                nb += 1
        assert nb == 14
        tiles = [(0, 1, 1, -1), (128, 129, 1, -1)] + [(256 + 128 * i, 259 + 128 * i, 0, i) for i in range(NTL)]

        if "ln" in stages:
            with ExitStack() as es2:
                C2 = Ctx(nc, es2)
                md = C2.sb([128, 4, KC], F32)
                P.dma(md[:], mods, (), [md])
                sc1 = C2.sb([128, 2, KC], F32)
                P.ts("dve", sc1[:, 0, :], md[:, 0, :], 1.0, None, ALU.add, None, [md], [sc1])
                P.ts("dve", sc1[:, 1, :], md[:, 2, :], 1.0, None, ALU.add, None, [md], [sc1])
                zt = C2.sb([128, KC, 2], BF16)
                P.memset("dve", zt, zt[:], 0.0)
                for pos in (0, 257, NTP - 1):
                    w_ = 2 if pos == 257 else 1
                    P.dma(hTd[:, :, pos:pos + w_], zt[:, :, 0:w_], [zt], [d_hT], q="pool", allow_slow_non_contiguous=True)
                TB = 256
                xfr = Ring([C2.sb([128, KC, TB], F32, "xf") for _ in range(2)])
                sq = C2.sb([128, KC, TB], F32, "sq")
                hbr = Ring([C2.sb([128, KC, TB], BF16, "hb") for _ in range(2)])
                st = [C2.sb([128, TB], F32, "st%d" % i) for i in range(4)]
                for t0 in range(0, NTOK, TB):
                    isctx = 1 if t0 < CTX else 0
                    xf = xfr.next()
                    hb = hbr.next()
                    P.dma(xf[:], xT[:, :, t0:t0 + TB], (), [xf])
                    emit_ln_fm(P, C2, K, xf, TB, KC, sc1[:, isctx, :], md[:, 1 + 2 * isctx, :], hb,
                               lambda kc: hb[:, kc, :], sq, PSF, st)
                    pos = t0 + 1 if isctx else t0 + 3
                    P.dma(hTd[:, :, pos:pos + TB], hb[:], [hb], [d_hT], q="pool")
                Sync.barrier(P)

        if "wprep" in stages:
            with ExitStack() as es2:
                C2 = Ctx(nc, es2)
                kk = min(8, KC)
                cwt = C2.sb([128, 3, 3, 512], F32, "cwt")
                for tap in range(3):
                    for gi in range(3):
                        P.dma(cwt[:, tap, gi, :], cw[tap, gi:gi + 1, :].to_broadcast([128, 512]), (), [cwt])
                wfr = Ring([C2.sb([128, kk, 512], F32, "wf") for _ in range(2)])
                wbr = Ring([C2.sb([128, kk, 512], BF16, "wb") for _ in range(3)])
                engs = ["dve", "pool"]
                ne = 0
                for g in range(8):
                    for kc0 in range(0, KC, kk):
                        wf = wfr.next()
                        P.dma(wf[:], wc[g, :, kc0:kc0 + kk, :], (), [wf])
                        if g in CONV_GROUPS:
                            gi = CONV_GROUPS.index(g)
                            for tap in range(3):
                                wb = wbr.next()
                                e = engs[ne % 2]
                                ne += 1
                                P.tt(e, wb[:], wf[:], bc_mid(cwt[:, tap, gi, :], kk), ALU.mult, [wf, cwt], [wb])
                                P.dma(wbd[blk_of[(g, tap)], :, kc0:kc0 + kk, :], wb[:], [wb], [d_wb], q="pool")
                        else:
                            wb = wbr.next()
                            e = engs[ne % 2]
                            ne += 1
                            P.copy(e, wb[:], wf[:], [wf], [wb])
                            P.dma(wbd[blk_of[(g, 0)], :, kc0:kc0 + kk, :], wb[:], [wb], [d_wb], q="pool")
                Sync.barrier(P)

        if "proj" in stages:
            with ExitStack() as es2:
                C2 = Ctx(nc, es2)
                cbt = C2.sb([128, 3, 512], F32, "cbt")
                for gi in range(3):
                    P.dma(cbt[:, gi, :], cb[0:1, gi, :].to_broadcast([128, 512]), (), [cbt])
                TBT = 4
                hsr = Ring([C2.sb([128, KC, TBT * 128 + 2], BF16, "hs") for _ in range(2)])
                wr = Ring([C2.sb([128, KC, 512], BF16, "w") for _ in range(3)])
                otr = Ring([C2.sb([128, 512], F32, "ot") for _ in range(3)])
                sbs = [tiles[0:2]] + [tiles[2 + i:2 + i + TBT] for i in range(0, NTL, TBT)]
                for sbt in sbs:
                    s0 = sbt[0][1]
                    nt = len(sbt)
                    hs = hsr.next()
                    P.dma(hs[:, :, 0:nt * 128 + 2], hTd[:, :, s0 - 1:s0 + nt * 128 + 1], [d_hT], [hs])
                    for g in range(8):
                        taps = list(range(3)) if g in CONV_GROUPS else [1]
                        pacc = [PSF.next() for _ in range(nt)]
                        for ti_, tap in enumerate(taps):
                            wt = wr.next()
                            P.dma(wt[:], wbd[blk_of[(g, tap if g in CONV_GROUPS else 0)]], [d_wb], [wt])
                            for ti in range(nt):
                                for kc in range(KC):
                                    P.mm(pacc[ti][:, :], hs[:, kc, ti * 128 + tap:ti * 128 + tap + 128], wt[:, kc, :],
                                         ti_ == 0 and kc == 0, ti_ == len(taps) - 1 and kc == KC - 1,
                                         [hs, wt], [pacc[ti]])
                        for ti in range(nt):
                            ot = otr.next()
                            if g in CONV_GROUPS:
                                gi = CONV_GROUPS.index(g)
                                P.tt("dve", ot[:], pacc[ti][:, :], cbt[:, gi, :], ALU.add, [pacc[ti], cbt], [ot])
                                if g == 3:
                                    silu_inplace(P, C2, ot, ot[:], otr.next())
                            else:
                                P.copy("act", ot[:], pacc[ti][:, :], [pacc[ti]], [ot])
                            tok0 = sbt[ti][0]
                            P.dma(ptm[tok0:tok0 + 128, g * 512:(g + 1) * 512], ot[:], [ot], [d_ptm], q="pool")
                Sync.barrier(P)

        env = dict(nc=nc, P=P, cf=cf, cbf=cbf, cm=cm, PSF=PSF, PSB=PSB, ptm=ptm, Y=Y, d_ptm=d_ptm, d_Y=d_Y,
                   tiles=tiles, NTL=NTL, NTA=NTA, LAT=LAT, KC=KC)
        if "ret" in stages:
            mix_ret(env, ret_dec, ret_pos, ret_cs)
            Sync.barrier(P)
        if "gla" in stages:
            mix_gla(env, gla_w2, gla_b2, gla_nw)
            Sync.barrier(P)
        if "ssd" in stages:
            mix_ssd(env, ssd_pr, ssd_row)
            Sync.barrier(P)
        if "hy" in stages:
            mix_hyena(env, hy_w1, hy_w2, hy_w3, hy_bf, hy_w4, hy_skip, hy_negd, zx, zc, tx, tcx, kxx, kxc, d_kx)
            Sync.barrier(P)
        if not fz:
            P.finish([d_Y])
            print("mix instructions", P.n_inst)
            P.close()
    return nc


def mix_scratch(C, KC, LAT):
    NTOK = CTX + LAT
    return (C.dscr("hTd", [128, KC, NTOK + 4], BF16), C.dscr("wbd", [14, 128, KC, 512], BF16),
            C.dscr("ptm", [NTOK, 4096], F32), C.dscr("kxx", [512, 2 * LAT], BF16),
            C.dscr("kxc", [512, 2 * CTX], BF16))


def silu_inplace(P, C, buf, ap, tmpbuf):
    shp = list(ap.shape)
    t = tmpbuf
    tv = t[:, 0:shp[1]] if len(shp) == 2 else t[:]
    P.act(tv, ap, AF.Sigmoid, [buf], [t])
    P.tt("dve", ap, ap, tv, ALU.mult, [buf, t], [buf])


def row_rsqrt_mean(P, W, y_buf, y_ap, F_, rstd_buf):
    nc = P.nc
    junk = W["junk"]
    ss = W["ss"]
    P.act(junk[:, 0:F_], y_ap, AF.Square, [y_buf], [junk])
    P.op("dve", lambda: nc.vector.reduce_sum(out=ss[:, 0:1], in_=junk[:, 0:F_], axis=AX.X), [junk], [ss])
    emit_rsqrt(P, rstd_buf, rstd_buf[:, 0:1], ss, ss[:, 0:1], EPS, 1.0 / F_)


def cla_step(env, W, QdT, KdT, QsT, ndk, Vb, V_ap, Keb, Ke_ap, maskb, mask_ap, rowscale, dec_ap, decb,
             S, Sb, s_sl, dv, ybuf, y_ap, first):
    P, PSF = env["P"], env["PSF"]
    pa = PSF.next()
    for kc in range(ndk):
        P.mm(pa[:, 0:128], KdT[1](kc), QdT[1](kc), kc == 0, kc == ndk - 1, [KdT[0], QdT[0]], [pa])
    am = W["am"].next()
    P.tt("dve", am[:], pa[:, 0:128], mask_ap, ALU.mult, [pa, maskb], [am])
    py1 = PSF.next()
    P.mm(py1[:, 0:dv], am[:], V_ap, True, True, [am, Vb], [py1])
    py2 = PSF.next()
    for kc in range(ndk):
        P.mm(py2[:, 0:dv], QsT[1](kc), Sb[:, kc, s_sl], kc == 0, kc == ndk - 1, [QsT[0], Sb], [py2])
    y1 = W["y1"].next()
    P.copy("act", y1[:, 0:dv], py1[:, 0:dv], [py1], [y1])
    if first:
        tgt, tb = y_ap, ybuf
    else:
        tb = W["y2"].next()
        tgt = tb[:, 0:dv]
    if rowscale is not None:
        P.stt("dve", tgt, py2[:, 0:dv], rowscale[1], y1[:, 0:dv], ALU.mult, ALU.add, [py2, y1, rowscale[0]], [tb])
    else:
        P.tt("dve", tgt, py2[:, 0:dv], y1[:, 0:dv], ALU.add, [py2, y1], [tb])
    if not first:
        P.tt("pool", y_ap, y_ap, tgt, ALU.add, [ybuf, tb], [ybuf])
    for mc in range(ndk):
        ps = PSF.next()
        P.mm(ps[:, 0:dv], Ke_ap(mc), V_ap, True, True, [Keb, Vb], [ps])
        P.stt("dve", S[:, mc, s_sl], S[:, mc, s_sl], dec_ap(mc), ps[:, 0:dv], ALU.mult, ALU.add, [S, decb, ps], [S])
        P.copy("act", Sb[:, mc, s_sl], S[:, mc, s_sl], [S], [Sb])


def tile_order(env, d):
    t = env["tiles"]
    if d == 0:
        return list(range(len(t)))
    return [1, 0] + list(range(len(t) - 1, 1, -1))


def transpose_bf(env, W, src_buf, src_ap, dst_buf, dst_ap):
    P, PSB, cbf = env["P"], env["PSB"], env["cbf"]
    pt = PSB.next()
    P.tr(pt[:, 0:128], src_ap, cbf[:, 0, :], [src_buf, cbf], [pt])
    P.copy("act", dst_ap, pt[:, 0:128], [pt], [dst_buf])


def _mk_work(C2, dv):
    return {"am": Ring([C2.sb([128, 128], BF16, "am") for _ in range(4)]),
            "y1": Ring([C2.sb([128, 256], F32, "y1") for _ in range(4)]),
            "y2": Ring([C2.sb([128, 256], F32, "y2") for _ in range(4)]),
            "junk": C2.sb([128, 256], F32, "junk"), "ss": C2.sb([128, 1], F32, "ss")}


def finish_tile(env, W, C2, ybuf, ti, gate_col, out_col, kind, roww=None, pre=None):
    P, nc, ptm, Y = env["P"], env["nc"], env["ptm"], env["Y"]
    tok0 = env["tiles"][ti][0]
    g = W["g"].next()
    P.dma(g[:], ptm[tok0:tok0 + 128, gate_col:gate_col + 256], [env["d_ptm"]], [g])
    y = ybuf[:, ti, :]
    if pre is not None:
        pre(y, tok0)
    if kind == "ln":
        m = W["m"]
        P.op("dve", lambda: nc.vector.reduce_sum(out=m[:, 0:1], in_=y, axis=AX.X), [ybuf], [m])
        P.ts("dve", m[:, 0:1], m[:, 0:1], -1.0 / 256, None, ALU.mult, None, [m], [m])
        P.ts("dve", y, y, m[:, 0:1], None, ALU.add, None, [ybuf, m], [ybuf])
    rs = W["rs"]
    row_rsqrt_mean(P, W, ybuf, y, 256, rs)
    sg = W["sg"].next()
    P.act(sg[:], g[:], AF.Sigmoid, [g], [sg])
    P.tt("pool", g[:], g[:], sg[:], ALU.mult, [g, sg], [g])
    if roww is not None:
        P.tt("pool", g[:], g[:], roww[1], ALU.mult, [g, roww[0]], [g])
    ob = W["ob"].next()
    P.stt("dve", ob[:], y, rs[:, 0:1], g[:], ALU.mult, ALU.mult, [ybuf, rs, g], [ob])
    P.dma(Y[tok0:tok0 + 128, out_col:out_col + 256], ob[:], [ob], [env["d_Y"]], q="pool")


def _fin_work(C2, W):
    W["g"] = Ring([C2.sb([128, 256], F32, "g") for _ in range(2)])
    W["sg"] = Ring([C2.sb([128, 256], F32, "sg") for _ in range(2)])
    W["ob"] = Ring([C2.sb([128, 256], BF16, "ob") for _ in range(2)])
    W["m"] = C2.sb([128, 1], F32, "m")
    W["rs"] = C2.sb([128, 1], F32, "rs")


def mix_ret(env, ret_dec, ret_pos, ret_cs):
    nc, P, cf, cm, ptm = env["nc"], env["P"], env["cf"], env["cm"], env["ptm"]
    NTA, tiles = env["NTA"], env["tiles"]
    with ExitStack() as es2:
        C2 = Ctx(nc, es2)
        W = _mk_work(C2, 256)
        _fin_work(C2, W)
        ybuf = C2.sb([128, NTA, 256], F32, "ybuf")
        lam = C2.sb([128, 2], F32, "lam")
        P.dma(lam[:], ret_dec.to_broadcast([128, 2]), (), [lam])
        P.act(lam[:], lam[:], AF.Exp, [lam], [lam])
        P.ts("dve", lam[:], lam[:], -1.0, None, ALU.mult, None, [lam], [lam])
        pos = C2.sb([128, 4], F32, "pos")
        P.dma(pos[:], ret_pos, (), [pos])
        fs = C2.sb([128, 2], F32, "fs")
        te = C2.sb([128, 2], F32, "te")
        dc = C2.sb([128, 2], F32, "dc")
        mk = C2.sb([128, 2, 128], F32, "mk")
        for d in range(2):
            P.act(fs[:, d:d + 1], pos[:, 2 * d:2 * d + 1], AF.Exp, [pos, lam], [fs], scale=lam[:, d:d + 1])
            P.act(te[:, d:d + 1], pos[:, 2 * d + 1:2 * d + 2], AF.Exp, [pos, lam], [te], scale=lam[:, d:d + 1])
            P.act(dc[:, d:d + 1], lam[:, d:d + 1], AF.Exp, [lam], [dc], scale=128.0)
            P.act(mk[:, d, :], cm(CI_LAGF + d), AF.Exp, [cf, lam], [mk], scale=lam[:, d:d + 1])
            P.tt("dve", mk[:, d, :], mk[:, d, :], cm(CI_U + d), ALU.mult, [mk, cf], [mk])
        P.ts("dve", te[:], te[:], 1.0 / 16, None, ALU.mult, None, [te], [te])
        S = C2.sb([128, 2, 256], F32, "S")
        Sb = C2.sb([128, 2, 256], BF16, "Sb")
        S2 = C2.sb([128, 2, 256], F32, "S2")
        Sb2 = C2.sb([128, 2, 256], BF16, "Sb2")
        rawr = Ring([C2.sb([128, 768], F32, "raw") for _ in range(4)])
        csr = Ring([C2.sb([128, 2, 256], F32, "cs") for _ in range(4)])
        rot = Ring([C2.sb([128, 2, 256], F32, "rot") for _ in range(4)])
        tmpr = Ring([C2.sb([128, 256], F32, "tmp") for _ in range(4)])
        qbr = Ring([C2.sb([128, 4, 256], BF16, "qb") for _ in range(4)])
        qtr = Ring([C2.sb([128, 4, 128], BF16, "qt") for _ in range(4)])
        inited = set()
        Ss, Sbs = [S, S2], [Sb, Sb2]

        def run_dir(d):
            S, Sb = Ss[d], Sbs[d]
            P.memset("dve", S, S[:], 0.0)
            P.memset("dve", Sb, Sb[:], 0.0)
            for ti in tile_order(env, d):
                tok0, _, isctx, lt = tiles[ti]
                first = ti not in inited
                inited.add(ti)
                raw = rawr.next()
                P.dma(raw[:], ptm[tok0:tok0 + 128, 3072:3840], [env["d_ptm"]], [raw])
                qb = qbr.next()
                if isctx:
                    qsrc, ksrc, qkb = raw[:, 0:256], raw[:, 256:512], raw
                else:
                    cs = csr.next()
                    P.dma(cs[:], ret_cs[lt * 128:(lt + 1) * 128], (), [cs])
                    ro = rot.next()
                    for w_ in range(2):
                        x = raw[:, w_ * 256:(w_ + 1) * 256]
                        tm = tmpr.next()
                        for g in range(2):
                            b0 = g * 128
                            P.tt("pool", tm[:, b0:b0 + 64], raw[:, w_ * 256 + b0 + 64:w_ * 256 + b0 + 128],
                                 cs[:, 1, b0:b0 + 64], ALU.mult, [raw, cs], [tm])
                            P.tt("pool", tm[:, b0 + 64:b0 + 128], raw[:, w_ * 256 + b0:w_ * 256 + b0 + 64],
                                 cs[:, 1, b0 + 64:b0 + 128], ALU.mult, [raw, cs], [tm])
                        P.tt("dve", ro[:, w_, :], x, cs[:, 0, :], ALU.mult, [raw, cs], [ro])
                        P.tt("dve", ro[:, w_, :], ro[:, w_, :], tm[:], ALU.add, [ro, tm], [ro])
                    qsrc, ksrc, qkb = ro[:, 0, :], ro[:, 1, :], ro
                P.copy("act", qb[:, 0, :], qsrc, [qkb], [qb])
                P.ts("dve", qb[:, 1, :], ksrc, 1.0 / 16, None, ALU.mult, None, [qkb], [qb])
                P.copy("act", qb[:, 2, :], raw[:, 512:768], [raw], [qb])
                P.ts("dve", qb[:, 3, :], ksrc, te[:, d:d + 1], None, ALU.mult, None, [qkb, te], [qb])
                qt = qtr.next()
                for w_ in range(2):
                    for kc in range(2):
                        transpose_bf(env, W, qb, qb[:, w_, kc * 128:(kc + 1) * 128], qt, qt[:, w_ * 2 + kc, :])
                cla_step(env, W, (qt, lambda kc: qt[:, kc, :]), (qt, lambda kc: qt[:, 2 + kc, :]),
                         (qt, lambda kc: qt[:, kc, :]), 2, qb, qb[:, 2, :], qb,
                         lambda mc: qb[:, 3, mc * 128:(mc + 1) * 128], mk, mk[:, d, :],
                         (fs, fs[:, d:d + 1]), lambda mc: dc[:, d:d + 1], dc, S, Sb, slice(0, 256), 256,
                         ybuf, ybuf[:, ti, :], first)
                yield
        gens = [run_dir(0), run_dir(1)]
        for _ in range(NTA):
            for g_ in gens:
                next(g_)
        for ti in range(NTA):
            finish_tile(env, W, C2, ybuf, ti, 3840, 768, "ln")


def mix_gla(env, gla_w2, gla_b2, gla_nw):
    nc, P, cf, cm, ptm, PSF = env["nc"], env["P"], env["cf"], env["cm"], env["ptm"], env["PSF"]
    NTA, tiles = env["NTA"], env["tiles"]
    with ExitStack() as es2:
        C2 = Ctx(nc, es2)
        W = _mk_work(C2, 256)
        _fin_work(C2, W)
        ybuf = C2.sb([128, NTA, 256], F32, "ybuf")
        w2 = C2.sb([16, 2, 128], F32, "w2")
        P.dma(w2[:], gla_w2, (), [w2])
        b2 = C2.sb([128, 2, 128], F32, "b2")
        P.dma(b2[:], gla_b2.to_broadcast([128, 2, 128]), (), [b2])
        nw = C2.sb([128, 256], F32, "nw")
        P.dma(nw[:], gla_nw.to_broadcast([128, 256]), (), [nw])
        onec = C2.sb([128, 1], F32, "onec")
        P.memset("dve", onec, onec[:], 1.0)
        S = C2.sb([128, 1, 256], F32, "S")
        Sb = C2.sb([128, 1, 256], BF16, "Sb")
        S2 = C2.sb([128, 1, 256], F32, "S2")
        Sb2 = C2.sb([128, 1, 256], BF16, "Sb2")
        rawr = Ring([C2.sb([128, 512], F32, "raw") for _ in range(4)])
        lrr = Ring([C2.sb([128, 32], F32, "lr") for _ in range(4)])
        lrt = Ring([C2.sb([16, 128], F32, "lrt") for _ in range(4)])
        spr = Ring([C2.sb([128, 128], F32, "sp") for _ in range(4)])
        exr = Ring([C2.sb([128, 4, 128], F32, "ex") for _ in range(4)])
        opr = Ring([C2.sb([128, 4, 128], BF16, "op") for _ in range(4)])
        otr = Ring([C2.sb([128, 3, 128], BF16, "opt") for _ in range(4)])
        vbr = Ring([C2.sb([128, 256], BF16, "vb") for _ in range(4)])
        dcr = Ring([C2.sb([128, 1], F32, "dc") for _ in range(4)])
        qsc = 128.0 ** -0.5
        inited = set()
        Ss, Sbs = [S, S2], [Sb, Sb2]

        def run_dir(d):
            S, Sb = Ss[d], Sbs[d]
            P.memset("dve", S, S[:], 0.0)
            P.memset("dve", Sb, Sb[:], 0.0)
            for ti in tile_order(env, d):
                tok0 = tiles[ti][0]
                first = ti not in inited
                inited.add(ti)
                raw = rawr.next()
                P.dma(raw[:], ptm[tok0:tok0 + 128, 2048:2560], [env["d_ptm"]], [raw])
                lr = lrr.next()
                P.dma(lr[:], ptm[tok0:tok0 + 128, 2560 + 264:2560 + 296], [env["d_ptm"]], [lr])
                pt = PSF.next()
                P.tr(pt[0:16, 0:128], lr[:, d * 16:(d + 1) * 16], cm(CI_ID), [lr, cf], [pt])
                lt_ = lrt.next()
                P.copy("act", lt_[:], pt[0:16, 0:128], [pt], [lt_])
                pl = PSF.next()
                P.mm(pl[:, 0:128], lt_[:], w2[:, d, :], True, True, [lt_, w2], [pl])
                sp = spr.next()
                P.tt("dve", sp[:], pl[:, 0:128], b2[:, d, :], ALU.add, [pl, b2], [sp])
                P.act(sp[:], sp[:], AF.Exp, [sp], [sp], scale=-1.0)
                P.act(sp[:], sp[:], AF.Ln, [sp], [sp], bias=onec[:, 0:1], scale=1.0)
                ex = exr.next()
                base = CI_G1F + 3 * d
                pg = [PSF.next() for _ in range(3)]
                for k_ in range(3):
                    P.mm(pg[k_][:, 0:128], cm(base + k_), sp[:], True, True, [cf, sp], [pg[k_]])
                P.act(ex[:, 0, :], pg[0][:, 0:128], AF.Exp, [pg[0]], [ex])
                P.act(ex[:, 1, :], pg[0][:, 0:128], AF.Exp, [pg[0]], [ex], scale=-1.0)
                P.act(ex[:, 2, :], pg[1][:, 0:128], AF.Exp, [pg[1]], [ex])
                P.act(ex[:, 3, :], pg[2][:, 0:128], AF.Exp, [pg[2]], [ex])
                pd = PSF.next()
                P.mm(pd[:, 0:1], sp[:], onec[:, 0:1], True, True, [sp, onec], [pd])
                dc = dcr.next()
                P.act(dc[:], pd[:, 0:1], AF.Exp, [pd], [dc], scale=-1.0 / 16)
                op = opr.next()
                q, k = raw[:, 0:128], raw[:, 128:256]
                P.stt("dve", op[:, 0, :], q, qsc, ex[:, 0, :], ALU.mult, ALU.mult, [raw, ex], [op])
                P.tt("pool", op[:, 1, :], k, ex[:, 1, :], ALU.mult, [raw, ex], [op])
                P.stt("dve", op[:, 2, :], q, qsc, ex[:, 2, :], ALU.mult, ALU.mult, [raw, ex], [op])
                P.tt("pool", op[:, 3, :], k, ex[:, 3, :], ALU.mult, [raw, ex], [op])
                vb = vbr.next()
                P.copy("act", vb[:], raw[:, 256:512], [raw], [vb])
                ot = otr.next()
                for k_ in range(3):
                    transpose_bf(env, W, op, op[:, k_, :], ot, ot[:, k_, :])
                cla_step(env, W, (ot, lambda kc: ot[:, 0, :]), (ot, lambda kc: ot[:, 1, :]),
                         (ot, lambda kc: ot[:, 2, :]), 1, vb, vb[:], op, lambda mc: op[:, 3, :],
                         cf, cm(CI_U + d), None, lambda mc: dc[:, 0:1], dc, S, Sb, slice(0, 256), 256,
                         ybuf, ybuf[:, ti, :], first)
                yield
        gens = [run_dir(0), run_dir(1)]
        for _ in range(NTA):
            for g_ in gens:
                next(g_)
        for ti in range(NTA):
            finish_tile(env, W, C2, ybuf, ti, 2560, 512, "rms", roww=(nw, nw[:]))


def mix_ssd(env, ssd_pr, ssd_row):
    nc, P, cf, cm, ptm, PSF = env["nc"], env["P"], env["cf"], env["cm"], env["ptm"], env["PSF"]
    NTA, tiles = env["NTA"], env["tiles"]
    with ExitStack() as es2:
        C2 = Ctx(nc, es2)
        W = _mk_work(C2, 64)
        _fin_work(C2, W)
        ybuf = C2.sb([128, NTA, 256], F32, "ybuf")
        pr = C2.sb([128, 4, 8], F32, "pr")
        P.dma(pr[:], ssd_pr.to_broadcast([128, 4, 8]), (), [pr])
        P.act(pr[:, 0, :], pr[:, 0, :], AF.Exp, [pr], [pr])
        P.ts("dve", pr[:, 0, :], pr[:, 0, :], -1.0, None, ALU.mult, None, [pr], [pr])
        rw = C2.sb([128, 2, 256], F32, "rw")
        P.dma(rw[:], ssd_row.to_broadcast([128, 2, 256]), (), [rw])
        onec = C2.sb([128, 1], F32, "onec")
        P.memset("dve", onec, onec[:], 1.0)
        S = C2.sb([128, 1, 256], F32, "S")
        Sb = C2.sb([128, 1, 256], BF16, "Sb")
        S2 = C2.sb([128, 1, 256], F32, "S2")
        Sb2 = C2.sb([128, 1, 256], BF16, "Sb2")
        rawr = Ring([C2.sb([128, 512], F32, "raw") for _ in range(4)])
        dtr = Ring([C2.sb([128, 3, 8], F32, "dt") for _ in range(4)])
        bcb = Ring([C2.sb([128, 3, 128], BF16, "bcb") for _ in range(4)])
        bct = Ring([C2.sb([128, 2, 128], BF16, "bct") for _ in range(4)])
        scr = Ring([C2.sb([128, 128], F32, "sc") for _ in range(4)])
        acr = Ring([C2.sb([128, 4, 4], F32, "ac") for _ in range(4)])
        abr = Ring([C2.sb([128, 128], F32, "ab") for _ in range(6)])
        mkr = Ring([C2.sb([128, 128], F32, "mk") for _ in range(6)])
        alr = Ring([C2.sb([128, 2], F32, "al") for _ in range(4)])
        vbr = Ring([C2.sb([128, 256], BF16, "vb") for _ in range(4)])
        ker = Ring([C2.sb([128, 128], BF16, "ke") for _ in range(6)])
        inited = set()
        Ss, Sbs = [S, S2], [Sb, Sb2]

        def run_dir(d):
            S, Sb = Ss[d], Sbs[d]
            P.memset("dve", S, S[:], 0.0)
            P.memset("dve", Sb, Sb[:], 0.0)
            Ud = CI_U + d
            endcol = 127 if d == 0 else 0
            for ti in tile_order(env, d):
                tok0 = tiles[ti][0]
                first = ti not in inited
                inited.add(ti)
                raw = rawr.next()
                P.dma(raw[:], ptm[tok0:tok0 + 128, 1536:2048], [env["d_ptm"]], [raw])
                dt = dtr.next()
                P.dma(dt[:, 0, :], ptm[tok0:tok0 + 128, 2560 + 256:2560 + 264], [env["d_ptm"]], [dt])
                P.tt("dve", dt[:, 0, :], dt[:, 0, :], pr[:, 1, :], ALU.add, [dt, pr], [dt])
                P.act(dt[:, 0, :], dt[:, 0, :], AF.Exp, [dt], [dt])
                P.act(dt[:, 0, :], dt[:, 0, :], AF.Ln, [dt], [dt], bias=onec[:, 0:1], scale=1.0)
                P.tt("dve", dt[:, 1, :], dt[:, 0, :], pr[:, 0, :], ALU.mult, [dt, pr], [dt])
                bc = bcb.next()
                P.copy("act", bc[:, 0, :], raw[:, 256:384], [raw], [bc])
                P.copy("act", bc[:, 1, :], raw[:, 384:512], [raw], [bc])
                bt = bct.next()
                transpose_bf(env, W, bc, bc[:, 0, :], bt, bt[:, 0, :])
                transpose_bf(env, W, bc, bc[:, 1, :], bt, bt[:, 1, :])
                pa = PSF.next()
                P.mm(pa[:, 0:128], bt[:, 0, :], bt[:, 1, :], True, True, [bt], [pa])
                sc = scr.next()
                P.copy("act", sc[:], pa[:, 0:128], [pa], [sc])
                pc = PSF.next()
                P.mm(pc[:, 0:4], cm(Ud), dt[:, 1, d * 4:d * 4 + 4], True, True, [cf, dt], [pc])
                ac = acr.next()
                P.copy("act", ac[:, 0, :], pc[:, 0:4], [pc], [ac])
                P.ts("dve", ac[:, 1, :], pc[:, 0:4], -1.0, None, ALU.mult, None, [pc], [ac])
                P.act(ac[:, 2, :], pc[:, 0:4], AF.Exp, [pc], [ac])
                vb = vbr.next()
                for h in range(4):
                    P.ts("dve", vb[:, h * 64:(h + 1) * 64], raw[:, h * 64:(h + 1) * 64],
                         dt[:, 0, d * 4 + h:d * 4 + h + 1], None, ALU.mult, None, [raw, dt], [vb])
                for h in range(4):
                    ab = abr.next()
                    P.ts("pool", ab[:], cm(CI_ONES), dt[:, 1, d * 4 + h:d * 4 + h + 1], None, ALU.mult, None,
                         [cf, dt], [ab])
                    pb = PSF.next()
                    P.mm(pb[:, 0:128], ab[:], cm(Ud), True, False, [ab, cf], [pb])
                    P.mm(pb[:, 0:128], cm(CI_ID), cm(CI_NEGF + d), False, True, [cf], [pb])
                    mk = mkr.next()
                    P.act(mk[:], pb[:, 0:128], AF.Exp, [pb, ac], [mk], bias=ac[:, 1, h:h + 1], scale=1.0)
                    al = alr.next()
                    P.copy("act", al[:, 0:1], pb[:, endcol:endcol + 1], [pb], [al])
                    P.act(al[:, 1:2], al[:, 0:1], AF.Exp, [al], [al])
                    P.act(ac[:, 3, h:h + 1], ac[:, 0, h:h + 1], AF.Exp, [ac, al], [ac], bias=al[:, 0:1], scale=-1.0)
                    ke = ker.next()
                    P.ts("dve", ke[:], raw[:, 256:384], ac[:, 3, h:h + 1], None, ALU.mult, None, [raw, ac], [ke])
                    am = W["am"].next()
                    P.tt("dve", am[:], sc[:], mk[:], ALU.mult, [sc, mk], [am])
                    sl = slice(h * 64, (h + 1) * 64)
                    py1 = PSF.next()
                    P.mm(py1[:, 0:64], am[:], vb[:, sl], True, True, [am, vb], [py1])
                    py2 = PSF.next()
                    P.mm(py2[:, 0:64], bt[:, 1, :], Sb[:, 0, sl], True, True, [bt, Sb], [py2])
                    y1 = W["y1"].next()
                    P.copy("act", y1[:, 0:64], py1[:, 0:64], [py1], [y1])
                    yap = ybuf[:, ti, sl]
                    if first:
                        P.stt("dve", yap, py2[:, 0:64], ac[:, 2, h:h + 1], y1[:, 0:64], ALU.mult, ALU.add,
                              [py2, ac, y1], [ybuf])
                    else:
                        y2 = W["y2"].next()
                        P.stt("dve", y2[:, 0:64], py2[:, 0:64], ac[:, 2, h:h + 1], y1[:, 0:64], ALU.mult, ALU.add,
                              [py2, ac, y1], [y2])
                        P.tt("pool", yap, yap, y2[:, 0:64], ALU.add, [ybuf, y2], [ybuf])
                    ps = PSF.next()
                    P.mm(ps[:, 0:64], ke[:], vb[:, sl], True, True, [ke, vb], [ps])
                    P.stt("dve", S[:, 0, sl], S[:, 0, sl], al[:, 1:2], ps[:, 0:64], ALU.mult, ALU.add,
                          [S, al, ps], [S])
                    P.copy("act", Sb[:, 0, sl], S[:, 0, sl], [S], [Sb])
                yield
        gens = [run_dir(0), run_dir(1)]
        for _ in range(NTA):
            for g_ in gens:
                next(g_)
        xsr = Ring([C2.sb([128, 256], F32, "xs") for _ in range(4)])

        def pre(y, tok0):
            xs = xsr.next()
            P.dma(xs[:], ptm[tok0:tok0 + 128, 1536:1792], [env["d_ptm"]], [xs])
            P.tt("pool", xs[:], xs[:], rw[:, 0, :], ALU.mult, [xs, rw], [xs])
            P.tt("dve", y, y, xs[:], ALU.add, [ybuf, xs], [ybuf])
        for ti in range(NTA):
            finish_ssd(env, W, ybuf, ti, pre, rw)


def finish_ssd(env, W, ybuf, ti, pre, rw):
    P, nc, ptm, Y = env["P"], env["nc"], env["ptm"], env["Y"]
    tok0 = env["tiles"][ti][0]
    g = W["g"].next()
    P.dma(g[:], ptm[tok0:tok0 + 128, 1024 + 256:1024 + 512], [env["d_ptm"]], [g])
    y = ybuf[:, ti, :]
    pre(y, tok0)
    sg = W["sg"].next()
    P.act(sg[:], g[:], AF.Sigmoid, [g], [sg])
    P.tt("pool", g[:], g[:], sg[:], ALU.mult, [g, sg], [g])
    P.tt("dve", y, y, g[:], ALU.mult, [ybuf, g], [ybuf])
    rs = W["rs"]
    row_rsqrt_mean(P, W, ybuf, y, 256, rs)
    ob = W["ob"].next()
    P.stt("dve", ob[:], y, rs[:, 0:1], rw[:, 1, :], ALU.mult, ALU.mult, [ybuf, rs, rw], [ob])
    P.dma(Y[tok0:tok0 + 128, 256:512], ob[:], [ob], [env["d_Y"]], q="pool")


def mix_hyena(env, hy_w1, hy_w2, hy_w3, hy_bf, hy_w4, hy_skip, hy_negd, zx, zc, tx, tcx, kxx, kxc, d_kx):
    nc, P, cf, cbf, ptm, PSF, Y = env["nc"], env["P"], env["cf"], env["cbf"], env["ptm"], env["PSF"], env["Y"]
    NTL, LAT = env["NTL"], env["LAT"]
    with ExitStack() as es2:
        C2 = Ctx(nc, es2)
        w1 = C2.sb([33, 64], F32, "w1"); P.dma(w1[:], hy_w1, (), [w1])
        w2 = C2.sb([64, 64], F32, "w2"); P.dma(w2[:], hy_w2, (), [w2])
        w3 = C2.sb([64, 64], F32, "w3"); P.dma(w3[:], hy_w3, (), [w3])
        bf = C2.sb([64, 6], F32, "bf"); P.dma(bf[:], hy_bf, (), [bf])
        w4 = C2.sb([64, 4, 256], F32, "w4"); P.dma(w4[:], hy_w4, (), [w4])
        negd = C2.sb([128, 2], F32, "negd"); P.dma(negd[:], hy_negd, (), [negd])
        fb = C2.sb([64, 3], F32, "fb")
        P.tt("dve", fb[:], bf[:, 0:3], bf[:, 3:6], ALU.mult, [bf], [fb])
        fsc = C2.sb([64, 4, 3], F32, "fsc")
        P.ts("dve", fsc[:, 0, :], bf[:, 3:6], 0.5, None, ALU.mult, None, [bf], [fsc])
        P.ts("dve", fsc[:, 1, :], fb[:], 0.5, None, ALU.mult, None, [fb], [fsc])
        P.ts("dve", fsc[:, 2, :], bf[:, 3:6], 0.25, None, ALU.mult, None, [bf], [fsc])
        P.ts("dve", fsc[:, 3, :], fb[:], 0.25, None, ALU.mult, None, [fb], [fsc])
        ztr = Ring([C2.sb([33, 512], F32, "zt") for _ in range(2)])
        trr = Ring([C2.sb([128, 512], F32, "tr") for _ in range(2)])
        ur = Ring([C2.sb([64, 512], F32, "u") for _ in range(2)])
        hr = Ring([C2.sb([64, 512], F32, "h") for _ in range(3)])
        winr = Ring([C2.sb([128, 512], F32, "win") for _ in range(2)])
        kbr = Ring([C2.sb([128, 512], BF16, "kb") for _ in range(3)])
        ws = [w1, w2, w3]
        for (L, zsrc, tsrc, kx) in ((LAT, zx, tx, kxx), (CTX, zc, tcx, kxc)):
            NB = min(512, L)
            for dirn in range(2):
                for blk in range(L // NB):
                    n0 = blk * NB
                    zt = ztr.next()
                    P.dma(zt[:, 0:NB], zsrc[dirn, :, n0:n0 + NB], (), [zt])
                    tr_ = trr.next()
                    P.dma(tr_[:, 0:NB], tsrc[dirn:dirn + 1, n0:n0 + NB].to_broadcast([128, NB]), (), [tr_])
                    hb, hap = zt, zt[:, 0:NB]
                    for l in range(3):
                        ps = PSF.next()
                        P.mm(ps[0:64, 0:NB], ws[l][:], hap, True, True, [ws[l], hb], [ps])
                        u = ur.next()
                        P.act(u[:, 0:NB], ps[0:64, 0:NB], AF.Sin, [ps, fsc], [u],
                              bias=fsc[:, 3, l:l + 1], scale=fsc[:, 2, l:l + 1])
                        hh = hr.next()
                        P.act(hh[:, 0:NB], ps[0:64, 0:NB], AF.Sin, [ps, fsc], [hh],
                              bias=fsc[:, 1, l:l + 1], scale=fsc[:, 0, l:l + 1])
                        P.tt("dve", u[:, 0:NB], u[:, 0:NB], u[:, 0:NB], ALU.mult, [u], [u])
                        P.ts("dve", u[:, 0:NB], u[:, 0:NB], -4.0, 2.0, ALU.mult, ALU.add, [u], [u])
                        P.tt("dve", hh[:, 0:NB], hh[:, 0:NB], u[:, 0:NB], ALU.mult, [hh, u], [hh])
                        hb, hap = hh, hh[:, 0:NB]
                    for o in range(2):
                        for cc in range(2):
                            ps = PSF.next()
                            P.mm(ps[:, 0:NB], w4[:, o * 2 + dirn, cc * 128:(cc + 1) * 128], hap, True, True,
                                 [w4, hb], [ps])
                            win = winr.next()
                            P.act(win[:, 0:NB], tr_[:, 0:NB], AF.Exp, [tr_, negd], [win], scale=negd[:, cc:cc + 1])
                            kb = kbr.next()
                            P.tt("dve", kb[:, 0:NB], ps[:, 0:NB], win[:, 0:NB], ALU.mult, [ps, win], [kb])
                            r0 = o * 256 + cc * 128
                            if dirn == 0:
                                c0, w_ = (L - 1) + n0, NB
                            else:
                                c0, w_ = n0, (NB - 1 if n0 + NB == L else NB)
                            P.dma(kx[r0:r0 + 128, c0:c0 + w_], kb[:, 0:w_], [kb], [d_kx], q="pool")
        Sync.barrier(P)
    CG = 32
    with ExitStack() as es2:
        C2 = Ctx(nc, es2)
        sk = C2.sb([128, 2, 256], F32, "sk")
        P.dma(sk[:], hy_skip.to_broadcast([128, 2, 256]), (), [sk])
        NJM = max(NTL, 2)
        bufs = {n_: C2.sb([128, NJM, CG], F32, n_) for n_ in ("v", "x1", "x2", "gt", "cv", "ta")}
        zb = C2.sb([128, NJM * CG], BF16, "zb")
        zr = C2.sb([128, NJM, CG], BF16, "zr")
        ob = C2.sb([128, NJM, CG], BF16, "ob")
        tring = Ring([C2.sb([128, 128 * (2 * NJM - 1)], BF16, "T") for _ in range(2)])
        for (NJ, tokbase, kx, L) in ((NTL, CTX, kxx, LAT), (2, 0, kxc, CTX)):
            ncols = 128 * (2 * NJ - 1)
            order = [NJ - 1] + [x for x in range(2 * NJ - 1) if x != NJ - 1]
            for cg in range(256 // CG):
                c0 = cg * CG
                for nm, col in (("v", 0), ("x1", 256), ("x2", 512), ("gt", 1024)):
                    b_ = bufs[nm]
                    src = ptm[tokbase:tokbase + 128 * NJ, col + c0:col + c0 + CG].rearrange("(J j) c -> j J c", j=128)
                    P.dma(b_[:, 0:NJ, :], src, [env["d_ptm"]], [b_])
                seq = [("v", "x1", "ta"), ("ta", "x2", "v")]
                for o in range(2):
                    zn, xn, on = seq[o]
                    z, xo, outb = bufs[zn], bufs[xn], bufs[on]
                    cv = bufs["cv"]
                    nfl = NJ * CG
                    P.copy("act", zb[:, 0:nfl], z[:, 0:NJ, :].rearrange("p j c -> p (j c)"), [z], [zb])
                    for f0 in range(0, nfl, 512):
                        fw = min(512, nfl - f0)
                        ps = PSF.next()
                        P.mm(ps[:, 0:fw], cbf[:, 1, :], zb[:, f0:f0 + fw], True, True, [cbf, zb], [ps])
                        P.copy("act", zr[:, 0:NJ, :].rearrange("p j c -> p (j c)")[:, f0:f0 + fw], ps[:, 0:fw],
                               [ps], [zr])
                    for c in range(CG):
                        T = tring.next()
                        row = o * 256 + c0 + c
                        src = bass.AP(kx.tensor, row * 2 * L, [[1, 128], [1, ncols]])
                        P.dma(T[:, 0:ncols], src, [d_kx], [T])
                        pc = PSF.next()
                        for idx, Dp in enumerate(order):
                            Dl = Dp - (NJ - 1)
                            I0 = max(0, Dl)
                            I1 = min(NJ - 1, NJ - 1 + Dl)
                            n = I1 - I0 + 1
                            J0 = I0 - Dl
                            P.mm(pc[:, I0:I1 + 1], T[:, 128 * Dp:128 * Dp + 128], zr[:, J0:J0 + n, c],
                                 idx == 0, idx == len(order) - 1, [T, zr], [pc])
                        P.copy("act", cv[:, 0:NJ, c], pc[:, 0:NJ], [pc], [cv])
                    skb = sk[:, o, c0:c0 + CG].unsqueeze(1).to_broadcast([128, NJ, CG])
                    P.tt("dve", outb[:, 0:NJ, :], z[:, 0:NJ, :], skb, ALU.mult, [z, sk], [outb])
                    P.tt("dve", cv[:, 0:NJ, :], cv[:, 0:NJ, :], outb[:, 0:NJ, :], ALU.add, [cv, outb], [cv])
                    P.tt("dve", outb[:, 0:NJ, :], xo[:, 0:NJ, :], cv[:, 0:NJ, :], ALU.mult, [xo, cv], [outb])
                z2, gt, ta = bufs["v"], bufs["gt"], bufs["ta"]
                P.act(ta[:, 0:NJ, :], gt[:, 0:NJ, :], AF.Sigmoid, [gt], [ta])
                P.tt("dve", gt[:, 0:NJ, :], gt[:, 0:NJ, :], ta[:, 0:NJ, :], ALU.mult, [gt, ta], [gt])
                P.tt("dve", ob[:, 0:NJ, :], z2[:, 0:NJ, :], gt[:, 0:NJ, :], ALU.mult, [z2, gt], [ob])
                dst = Y[tokbase:tokbase + 128 * NJ, c0:c0 + CG].rearrange("(J j) c -> j J c", j=128)
                P.dma(dst, ob[:, 0:NJ, :], [ob], [env["d_Y"]], q="pool")


IN_SIZES = (3072, 1024, 2048, 32, 1024, 512, 512, 1024, 32, 1024, 1024, 1024, 1024, 1024)
OFF = np.concatenate([[0], np.cumsum(IN_SIZES)]).astype(np.int64)


def _fm(a):
    a = np.asarray(a)
    return np.ascontiguousarray(a.T.reshape(-1, 128, a.shape[0]).transpose(1, 0, 2))


def _pk(v):
    return np.ascontiguousarray(np.asarray(v).reshape(-1, 128).T)


def core_cols(q):
    r = np.arange
    g = []
    g.append(np.concatenate([OFF[0] + 256 * q + r(256), OFF[0] + 1024 + 256 * q + r(256)]))
    g.append(np.concatenate([OFF[0] + 2048 + 256 * q + r(256), -np.ones(256, np.int64)]))
    g.append(np.concatenate([OFF[1] + 256 * q + r(256), OFF[4] + 256 * q + r(256)]))
    g.append(np.concatenate([OFF[2] + 256 * q + r(256), OFF[2] + 1024 + 128 * q + r(128),
                             OFF[2] + 1536 + 128 * q + r(128)]))
    g.append(np.concatenate([OFF[5] + 128 * q + r(128), OFF[6] + 128 * q + r(128), OFF[7] + 256 * q + r(256)]))
    dtc = np.array([OFF[3] + d * 16 + 4 * q + rr for d in range(2) for rr in range(4)])
    g.append(np.concatenate([OFF[9] + 256 * q + r(256), dtc, OFF[8] + r(32), -np.ones(216, np.int64)]))
    g.append(np.concatenate([OFF[10] + 256 * q + r(256), OFF[11] + 256 * q + r(256)]))
    g.append(np.concatenate([OFF[12] + 256 * q + r(256), OFF[13] + 256 * q + r(256)]))
    return np.concatenate(g).astype(np.int64)


def hy_tables(L):
    t = (np.arange(L, dtype=np.float32) / np.float32(L)).astype(np.float64)
    bands = np.arange(1, 17, dtype=np.float64)[None, :]
    z = np.concatenate([t[:, None], np.cos(2 * np.pi * bands * t[:, None]), np.sin(2 * np.pi * bands * t[:, None])], 1)
    zT = z.T.astype(np.float32)
    return np.stack([zT, zT[:, ::-1]]).copy(), np.stack([t, t[::-1]]).astype(np.float32).copy()


def rope_tables(LAT):
    t = np.arange(LAT)
    row = (t // 64).astype(np.float64)[:, None]
    col = (t % 64).astype(np.float64)[:, None]
    inv = 10000.0 ** (-np.arange(0, 128, 2, dtype=np.float64) / 128)[None, :]
    ar, ac = row * inv, col * inv
    cos = np.concatenate([np.cos(ar), np.cos(ar), np.cos(ac), np.cos(ac)], 1)
    sin = np.concatenate([-np.sin(ar), np.sin(ar), -np.sin(ac), np.sin(ac)], 1)
    return np.ascontiguousarray(np.stack([cos, sin], 1)).astype(np.float32)


_PROG = {}


def _prog(key, fn):
    if key not in _PROG:
        _PROG[key] = fn()
    return _PROG[key]


def run_ada(inp, KC):
    D = 128 * KC
    c_all = np.zeros((4, D), np.float32)
    c_all[0:2] = inp["c"]
    c_all[2] = inp["c_ctx"]
    chunks = [(l, n) for l in range(DEPTH) for n in range(3 * KC)]
    NCH = -(-len(chunks) // 8)
    nc = _prog(("ada", KC, NCH), lambda: build_ada(KC, NCH))
    cT = np.ascontiguousarray(c_all.T.reshape(KC, 128, 4).transpose(1, 0, 2))
    maps, owner = [], []
    for core in range(8):
        ws, bs, own = [], [], []
        for i in range(NCH):
            l, n = chunks[(core * NCH + i) % len(chunks)]
            ws.append(inp["w_ada"][l][:, n * 128:(n + 1) * 128].reshape(KC, 128, 128).transpose(1, 0, 2))
            bs.append(inp["b_ada"][l][n * 128:(n + 1) * 128])
            own.append((l, n))
        maps.append({"cT": cT, "w": np.ascontiguousarray(np.stack(ws)), "b": np.ascontiguousarray(np.stack(bs, 1))})
        owner.append(own)
    res = run_bass_kernel_spmd(nc, maps, core_ids=list(range(8)))
    mods = np.zeros((DEPTH, 3 * D, 4), np.float32)
    for core in range(8):
        o = res.results[core]["o"]
        for i, (l, n) in enumerate(owner[core]):
            mods[l, n * 128:(n + 1) * 128, :] = o[:, i, :]
    return mods


def mix_shared_inputs(LAT):
    zx, tx = hy_tables(LAT)
    zc, tcx = hy_tables(CTX)
    ii = np.arange(128, dtype=np.float32)
    ret_pos = np.stack([ii + 1, 127 - ii, 128 - ii, ii], 1).astype(np.float32)
    return {"constf": host_constf(), "constb": host_constb(), "zx": zx, "zc": zc, "tx": tx, "tcx": tcx,
            "ret_pos": ret_pos, "ret_cs": rope_tables(LAT)}


def mix_core_inputs(inp, l, q, KC):
    deltas = np.abs(np.linspace(math.log(1e-2) / 1.5, math.log(1e-2) / 0.3, 1024))
    cols = core_cols(q)
    w = inp["w_in"][l]
    wcore = np.where(cols[None, :] >= 0, w[:, np.maximum(cols, 0)], 0.0).astype(np.float32)
    wcm = np.ascontiguousarray(wcore.reshape(KC, 128, 8, 512).transpose(2, 1, 0, 3))
    cw = np.zeros((3, 3, 512), np.float32)
    cb = np.zeros((1, 3, 512), np.float32)
    hcw, hcb = inp["hy_conv_w"][l], inp["hy_conv_b"][l]
    scw, scb = inp["ssd_conv_w"][l], inp["ssd_conv_b"][l]
    r = np.arange
    i0 = np.concatenate([256 * q + r(256), 1024 + 256 * q + r(256)])
    i1 = 2048 + 256 * q + r(256)
    i3 = np.concatenate([256 * q + r(256), 1024 + 128 * q + r(128), 1536 + 128 * q + r(128)])
    cw[:, 0, :] = hcw[:, i0]; cb[0, 0, :] = hcb[i0]
    cw[:, 1, 0:256] = hcw[:, i1]; cb[0, 1, 0:256] = hcb[i1]
    cw[:, 2, :] = scw[:, i3]; cb[0, 2, :] = scb[i3]
    hy_bf = np.stack([inp["hy_b1"][l], inp["hy_b2"][l], inp["hy_b3"][l],
                      inp["hy_freq"][l][0], inp["hy_freq"][l][1], inp["hy_freq"][l][2]], 1)
    w4 = inp["hy_w4"][l].reshape(64, 2, 2, 1024)[:, :, :, q * 256:(q + 1) * 256].reshape(64, 4, 256)
    hd = [4 * q + rr for rr in range(4)]
    ssd_pr = np.zeros((1, 4, 8), np.float32)
    ssd_pr[0, 0] = inp["ssd_a_log"][l][:, hd].reshape(8)
    ssd_pr[0, 1] = inp["ssd_dt_bias"][l][:, hd].reshape(8)
    ssd_row = np.stack([np.repeat(inp["ssd_d"][l][hd], 64), inp["ssd_norm_w"][l][q * 256:(q + 1) * 256]])[None]
    return {
        "wc": wcm, "cw": cw, "cb": cb,
        "hy_w1": np.ascontiguousarray(inp["hy_w1"][l]), "hy_w2": np.ascontiguousarray(inp["hy_w2"][l]),
        "hy_w3": np.ascontiguousarray(inp["hy_w3"][l]), "hy_bf": np.ascontiguousarray(hy_bf.astype(np.float32)),
        "hy_w4": np.ascontiguousarray(w4),
        "hy_skip": np.ascontiguousarray(inp["hy_skip"][l][:, q * 256:(q + 1) * 256][None]),
        "hy_negd": np.ascontiguousarray((-deltas[q * 256:(q + 1) * 256]).reshape(2, 128).T.astype(np.float32)),
        "ssd_pr": ssd_pr, "ssd_row": np.ascontiguousarray(ssd_row.astype(np.float32)),
        "gla_w2": np.ascontiguousarray(inp["gla_w2"][l][:, :, q * 128:(q + 1) * 128].transpose(1, 0, 2)),
        "gla_b2": np.ascontiguousarray(inp["gla_b2"][l][:, q * 128:(q + 1) * 128][None]),
        "gla_nw": np.ascontiguousarray(inp["gla_norm_w"][l][None]),
        "ret_dec": np.ascontiguousarray(inp["ret_decay"][l][:, q][None]),
    }


def run_mix(inp, l, xcat, mods, KC, LAT, stages=None):
    D = 128 * KC
    kw = {} if stages is None else {"stages": stages}
    nc = _prog(("mix", KC, LAT, stages), lambda: build_mix(KC, LAT, **kw))
    shared = mix_shared_inputs(LAT)
    maps = []
    for core in range(8):
        b, q = core // 4, core % 4
        mm_ = mods[l]
        md = np.stack([_pk(mm_[D:2 * D, b]), _pk(mm_[0:D, b]), _pk(mm_[D:2 * D, 2]), _pk(mm_[0:D, 2])], 1)
        m = {"xT": _fm(xcat[b]), "mods": np.ascontiguousarray(md.astype(np.float32))}
        m.update(shared)
        m.update(mix_core_inputs(inp, l, q, KC))
        maps.append(m)
    res = run_bass_kernel_spmd(nc, maps, core_ids=list(range(8)))
    NTOK = CTX + LAT
    yfull = np.zeros((2, NTOK, 4, 4, 256), NPBF)
    for core in range(8):
        b, q = core // 4, core % 4
        Yc = res.results[core]["Y"]
        yfull[b, :, :, q, :] = Yc.reshape(NTOK, 4, 256)
    return yfull.reshape(2, NTOK, 4096)


def run_post(inp, l, xcat, yfull, mods, KC, LAT, with_ctx):
    D = 128 * KC
    NCT = CTX // 4 if with_ctx else 0
    NLT = LAT // 4
    nc = _prog(("post", KC, NCT, NLT), lambda: build_post(KC, NCT, NLT, TP=352))
    wg = np.ascontiguousarray(inp["w_gate"][l].reshape(4, KC, 128, KC, 128).transpose(0, 3, 2, 1, 4))
    wb = np.ascontiguousarray(inp["w_br"][l].reshape(4, 8, 128, KC, 128).transpose(0, 3, 2, 1, 4))
    wo = np.ascontiguousarray(inp["w_out"][l].reshape(KC, 128, KC, 128).transpose(2, 1, 0, 3))
    ones = np.ones((128, 128), np.float32)
    mm_ = mods[l]
    maps, toks = [], []
    for core in range(8):
        b, q = core // 4, core % 4
        idx = np.concatenate([NCT * q + np.arange(NCT), CTX + NLT * q + np.arange(NLT)]).astype(np.int64)
        toks.append((b, idx))
        md = np.stack([_pk(mm_[D:2 * D, b]), _pk(mm_[0:D, b]), _pk(mm_[2 * D:3 * D, b]),
                       _pk(mm_[D:2 * D, 2]), _pk(mm_[0:D, 2]), _pk(mm_[2 * D:3 * D, 2]),
                       _pk(inp["ln_g"][l]), _pk(inp["ln_b"][l])], 1).astype(np.float32)
        maps.append({"xT": _fm(xcat[b][idx]), "yT": _fm(yfull[b][idx]), "mods": np.ascontiguousarray(md),
                     "wg": wg, "wb": wb, "wo": wo, "onesf": ones})
    res = run_bass_kernel_spmd(nc, maps, core_ids=list(range(8)))
    xnew = [x.copy() for x in xcat]
    for core in range(8):
        b, idx = toks[core]
        o = res.results[core]["out"]
        xnew[b][idx] = o.transpose(1, 0, 2).reshape(D, len(idx)).T
    return xnew


def run_module(inp, KC, LAT):
    inp = {k: np.asarray(v) for k, v in inp.items()}
    mods = run_ada(inp, KC)
    xcat = [np.concatenate([inp["ctx"][b], inp["x"][b]], 0).astype(np.float32) for b in range(2)]
    for l in range(DEPTH):
        yfull = run_mix(inp, l, xcat, mods, KC, LAT)
        xcat = run_post(inp, l, xcat, yfull, mods, KC, LAT, with_ctx=(l < DEPTH - 1))
    return np.stack([xcat[b][CTX:] for b in range(2)]).astype(np.float32)


class YView:
    def __init__(self, yfull, q):
        self.y, self.q = yfull, q

    def __getitem__(self, key):
        rs, cs = key
        i, off = cs.start // 256, cs.start % 256
        st = i * 1024 + self.q * 256 + off
        return self.y[rs, st:st + (cs.stop - cs.start)]


def emit_ada_fused(nc, P, KC, fz, mods_mix, mods_post):
    with ExitStack() as es2:
        C = Ctx(nc, es2)
        NCH = DEPTH * 3 * KC
        cT = C.din("cT", [128, KC, 2])
        w = C.din("w_ada", [NCH, 128, KC, 128])
        b = C.din("b_ada", [128, NCH])
        ct = C.sb([128, KC, 2], F32)
        sg = C.sb([128, KC, 2], F32)
        st = C.sb([128, KC, 2], F32)
        bt = C.sb([128, NCH], F32)
        ot = C.sb([128, NCH, 2], F32)
        wr = Ring([C.sb([128, KC, 128], F32, "adaw") for _ in range(3)])
        P.dma(ct[:], cT, (), [ct])
        P.dma(bt[:], b, (), [bt])
        P.act(sg[:], ct[:], AF.Sigmoid, [ct], [sg])
        P.tt("dve", st[:], ct[:], sg[:], ALU.mult, [ct, sg], [st])
        for n in range(NCH):
            wt = wr.next()
            P.dma(wt[:], w[n], (), [wt])
            pp = fz["PSF"].next()
            for kc in range(KC):
                P.mm(pp[:, 0:2], wt[:, kc, :], st[:, kc, :], kc == 0, kc == KC - 1, [wt, st], [pp])
            P.act(ot[:, n, :], pp[:, 0:2], AF.Identity, [pp, bt], [ot], bias=bt[:, n:n + 1], scale=1.0)
        dm = fz["dbuf"]["mods"]
        for l in range(DEPTH):
            mmt = C.sb([128, 4, KC], F32)
            mpt = C.sb([128, 6, KC], F32)

            def col(s_, r):
                base = l * 3 * KC + s_ * KC
                return ot[:, base:base + KC, r]
            for j, (s_, r) in enumerate([(1, 0), (0, 0), (1, 1), (0, 1)]):
                P.copy("dve", mmt[:, j, :], col(s_, r), [ot], [mmt])
            for j, (s_, r) in enumerate([(1, 0), (0, 0), (2, 0), (1, 1), (0, 1), (2, 1)]):
                P.copy("dve", mpt[:, j, :], col(s_, r), [ot], [mpt])
            P.dma(mods_mix[l], mmt[:], [mmt], [dm], q="pool")
            P.dma(mods_post[l], mpt[:], [mpt], [dm], q="pool")
        Sync.barrier(P)


def build_fused(KC, LAT):
    NTOK = CTX + LAT
    nc = bass.Bass("TRN2", target_bir_lowering=False)
    with ExitStack() as es:
        C = Ctx(nc, es)
        P = Prog(nc)
        fz = dict(nc=nc, P=P, es=es, dins={}, sfx="")
        fz["PSF"] = Ring([C.ps([128, 512], F32, "psf") for _ in range(6)])
        fz["PSB"] = Ring([C.ps([128, 512], BF16, "psb") for _ in range(2)])
        hTd, wbd, ptm, kxx, kxc = mix_scratch(C, KC, LAT)
        fz["scr"] = dict(hTd=hTd, wbd=wbd, ptm=ptm, kxx=kxx, kxc=kxc)
        fz["dbuf"] = {k: Buf(None) for k in ("hT", "wb", "ptm", "kx", "Y", "mods", "xin", "xout")}
        x0 = C.din("xT0", [128, KC, NTOK])
        x1 = C.dscr("xT1", [128, KC, NTOK], F32)
        outF = C.dout("out", [128, KC, LAT])
        Yfull = C.dscr("Yfull", [NTOK, 4096], BF16)
        mods_mix = [C.dscr("mm%d" % l, [128, 4, KC], F32) for l in range(DEPTH)]
        mods_post = [C.dscr("mp%d" % l, [128, 6, KC], F32) for l in range(DEPTH)]
        emit_ada_fused(nc, P, KC, fz, mods_mix, mods_post)
        for l in range(DEPTH):
            xin = x0 if l == 0 else x1
            for q in range(4):
                fz.update(sfx="_%d_%d" % (l, q), xT=xin, mods=mods_mix[l], Y=YView(Yfull, q))
                build_mix(KC, LAT, fz=fz)
            lat_passes = [(CTX + 256 * i, 256, 0) for i in range(LAT // 256)]
            if l < DEPTH - 1:
                passes = [(0, 256, 1, 0)] + [(t, n, c_, t) for (t, n, c_) in lat_passes]
                outap = x1
            else:
                passes = [(t, n, c_, t - CTX) for (t, n, c_) in lat_passes]
                outap = outF
            fz.update(sfx="_%d" % l, xT=xin, mods_post=mods_post[l], Yfull=Yfull, out=outap, passes=passes)
            build_post(KC, 0, 0, fz=fz)
        P.finish([fz["dbuf"]["xout"]])
        print("fused instructions", P.n_inst)
        P.close()
    return nc


def run_fused(inp, KC, LAT):
    inp = {k: np.asarray(v) for k, v in inp.items()}
    D = 128 * KC
    nc = _prog(("fused", KC, LAT), lambda: build_fused(KC, LAT))
    common = dict(mix_shared_inputs(LAT))
    chunks = [(l, n) for l in range(DEPTH) for n in range(3 * KC)]
    common["w_ada"] = np.ascontiguousarray(np.stack(
        [inp["w_ada"][l][:, n * 128:(n + 1) * 128].reshape(KC, 128, 128).transpose(1, 0, 2) for (l, n) in chunks]))
    common["b_ada"] = np.ascontiguousarray(np.stack([inp["b_ada"][l][n * 128:(n + 1) * 128] for (l, n) in chunks], 1))
    for l in range(DEPTH):
        for q in range(4):
            for k_, v in mix_core_inputs(inp, l, q, KC).items():
                common["%s_%d_%d" % (k_, l, q)] = v
        common["wg_%d" % l] = np.ascontiguousarray(inp["w_gate"][l].reshape(4, KC, 128, KC, 128).transpose(0, 3, 2, 1, 4))
        common["wb_%d" % l] = np.ascontiguousarray(inp["w_br"][l].reshape(4, 8, 128, KC, 128).transpose(0, 3, 2, 1, 4))
        common["wo_%d" % l] = np.ascontiguousarray(inp["w_out"][l].reshape(KC, 128, KC, 128).transpose(2, 1, 0, 3))
        common["lnp_%d" % l] = np.ascontiguousarray(np.stack([_pk(inp["ln_g"][l]), _pk(inp["ln_b"][l])], 1).astype(np.float32))
    maps = []
    for b in range(2):
        m = dict(common)
        xc = np.concatenate([inp["ctx"][b], inp["x"][b]], 0).astype(np.float32)
        m["xT0"] = _fm(xc)
        cc = np.stack([inp["c"][b], inp["c_ctx"]], 1).astype(np.float32)
        m["cT"] = np.ascontiguousarray(cc.reshape(KC, 128, 2).transpose(1, 0, 2))
        maps.append(m)
    res = run_bass_kernel_spmd(nc, maps, core_ids=[0, 1])
    outs = []
    for b in range(2):
        o = res.results[b]["out"]
        outs.append(o.transpose(1, 0, 2).reshape(D, LAT).T)
    return np.stack(outs).astype(np.float32)


def kernel(**inputs):
    return run_module(inputs, 32, 8192)
```
